# Optimizing a Trainium2 kernel written in Bass

```python
import jax, jax.numpy as jnp
from jax import lax
import numpy as np

D_MODEL = 2048
BATCH = 4
SEQ = 4096
DEPTH = 1

HEAD_DIM = 128
HEADS_PER_GROUP = 4
DILATED_GROUPS = ((128, 1), (512, 4), (2048, 16))
N_ATT_GROUPS = 3
N_ATT_HEADS = N_ATT_GROUPS * HEADS_PER_GROUP
ATT_WIDTH = N_ATT_HEADS * HEAD_DIM
ATT_OUT_WIDTH = HEADS_PER_GROUP * HEAD_DIM
ROPE_DIM = HEAD_DIM // 4
ROPE_THETA = 500000.0
ATT_BLOCK = 128

SG_CHUNK = 128
SG_GROUPS = 12
SG_GROUP_DIM = 128
SG_WIDTH = SG_GROUPS * SG_GROUP_DIM

D_FF = 5632

NORM_EPS = 1e-6
LN_EPS = 1e-5
IN_WIDTH = 3 * ATT_WIDTH + 2 * SG_WIDTH + 2 * D_MODEL

kernel_name = "dilated_attn_gmlp_gated_macaron_layer"


def rmsnorm(x, g):
    x32 = x.astype(jnp.float32)
    r = x32 * lax.rsqrt(jnp.mean(x32 * x32, axis=-1, keepdims=True) + NORM_EPS)
    return (r * g.astype(jnp.float32)).astype(x.dtype)


def layernorm(x, g, b):
    x32 = x.astype(jnp.float32)
    mu = jnp.mean(x32, axis=-1, keepdims=True)
    var = jnp.mean(jnp.square(x32 - mu), axis=-1, keepdims=True)
    y = (x32 - mu) * lax.rsqrt(var + LN_EPS)
    return (y * g.astype(jnp.float32) + b.astype(jnp.float32)).astype(x.dtype)


def swiglu(x, w_gate, w_up, w_down):
    return (jax.nn.silu(x @ w_gate) * (x @ w_up)) @ w_down


def partial_rope(t, pos):
    half = ROPE_DIM // 2
    inv_freq = ROPE_THETA ** (-jnp.arange(0, ROPE_DIM, 2, dtype=jnp.float32) / ROPE_DIM)
    ang = pos.astype(jnp.float32)[:, None] * inv_freq[None, :]
    ang = ang.reshape((1, t.shape[1]) + (1,) * (t.ndim - 3) + (half,))
    cos = jnp.cos(ang).astype(t.dtype)
    sin = jnp.sin(ang).astype(t.dtype)
    x1 = t[..., :half]
    x2 = t[..., half:ROPE_DIM]
    return jnp.concatenate([x1 * cos - x2 * sin, x2 * cos + x1 * sin, t[..., ROPE_DIM:]], axis=-1)


def dilated_window_attention(q, k, v, window, dilation):
    b, s, h, hd = q.shape
    L = s // dilation
    n_blk = -(-L // ATT_BLOCK)
    Lp = n_blk * ATT_BLOCK
    reach = window // dilation

    def by_residue(t):
        t = t.reshape(b, L, dilation, h, hd).transpose(0, 3, 2, 1, 4)
        t = jnp.pad(t, ((0, 0), (0, 0), (0, 0), (0, Lp - L), (0, 0)))
        return t.reshape(b, h, dilation, n_blk, ATT_BLOCK, hd)

    qb, kb, vb = by_residue(q), by_residue(k), by_residue(v)

    def with_prev(t):
        prev = jnp.pad(t[:, :, :, :-1], ((0, 0), (0, 0), (0, 0), (1, 0), (0, 0), (0, 0)))
        return jnp.concatenate([prev, t], axis=4)

    kw, vw = with_prev(kb), with_prev(vb)
    scores = jnp.einsum('bhrnqd,bhrnkd->bhrnqk', qb, kw).astype(jnp.float32) * (hd ** -0.5)
    qi = jnp.arange(ATT_BLOCK)[None, :, None]
    kj = jnp.arange(2 * ATT_BLOCK)[None, None, :]
    blk = jnp.arange(n_blk)[:, None, None]
    diff = qi + ATT_BLOCK - kj
    key_idx = blk * ATT_BLOCK - ATT_BLOCK + kj
    valid = (diff >= 0) & (diff <= reach) & (key_idx >= 0)
    scores = jnp.where(valid[None, None, None], scores, -jnp.inf)
    m = jnp.max(scores, axis=-1, keepdims=True)
    p = jnp.exp(scores - m)
    l = jnp.sum(p, axis=-1, keepdims=True)
    o = jnp.einsum('bhrnqk,bhrnkd->bhrnqd', p.astype(vw.dtype), vw).astype(jnp.float32) / l

    o = o.reshape(b, h, dilation, Lp, hd)[:, :, :, :L].transpose(0, 3, 2, 1, 4).reshape(b, s, h, hd)

    def stat_back(t):
        t = t.reshape(b, h, dilation, Lp)[:, :, :, :L]
        return t.transpose(0, 3, 2, 1).reshape(b, s, h)

    return o, stat_back(m), stat_back(l)


def hybrid_mixer(h, w_in, sg_ln_g, sg_ln_b, sg_w, sg_b, w_att_out, w_sg_out, w_out):
    b, s, _ = h.shape
    proj = h @ w_in
    splits = np.cumsum([ATT_WIDTH, ATT_WIDTH, ATT_WIDTH, SG_WIDTH, SG_WIDTH, D_MODEL]).tolist()
    q, k, v, u, vs, g_att, g_sg = jnp.split(proj, splits, axis=-1)

    pos = jnp.arange(s)
    q = partial_rope(q.reshape(b, s, N_ATT_GROUPS, HEADS_PER_GROUP, HEAD_DIM), pos)
    k = partial_rope(k.reshape(b, s, N_ATT_GROUPS, HEADS_PER_GROUP, HEAD_DIM), pos)
    v = v.reshape(b, s, N_ATT_GROUPS, HEADS_PER_GROUP, HEAD_DIM)
    outs, maxes, dens = [], [], []
    for gi, (window, dilation) in enumerate(DILATED_GROUPS):
        o_g, m_g, l_g = dilated_window_attention(q[:, :, gi], k[:, :, gi], v[:, :, gi], window, dilation)
        outs.append(o_g)
        maxes.append(m_g)
        dens.append(l_g)
    o_all = jnp.stack(outs)
    m_all = jnp.stack(maxes)
    l_all = jnp.stack(dens)
    w_den = l_all * jnp.exp(m_all - jnp.max(m_all, axis=0, keepdims=True))
    o_att = jnp.sum(w_den[..., None] * o_all, axis=0) / jnp.sum(w_den, axis=0)[..., None]
    y_att = o_att.astype(h.dtype).reshape(b, s, ATT_OUT_WIDTH) @ w_att_out

    u = jax.nn.gelu(u, approximate=False)
    vs = layernorm(jax.nn.gelu(vs, approximate=False), sg_ln_g, sg_ln_b)
    vc = vs.reshape(b, s // SG_CHUNK, SG_CHUNK, SG_GROUPS, SG_GROUP_DIM)
    causal = jnp.tril(jnp.ones((SG_CHUNK, SG_CHUNK), dtype=sg_w.dtype))
    w_sp = sg_w * causal[None]
    spatial = jnp.einsum('gts,bcsgd->bctgd', w_sp, vc) + sg_b.T[None, None, :, :, None]
    y_sg = (u * spatial.reshape(b, s, SG_WIDTH)) @ w_sg_out

    merged = jax.nn.sigmoid(g_att) * y_att + jax.nn.sigmoid(g_sg) * y_sg
    return merged @ w_out


def setup_inputs(seed: int = 0) -> dict:
    key = jax.random.key(seed)
    ks = jax.random.split(key, 24)
    f32 = jnp.float32

    def nrm(k, shape, fan_in):
        return jax.random.normal(k, shape, f32) * (fan_in ** -0.5)

    def gain(k, shape):
        return 1.0 + 0.05 * jax.random.normal(k, shape, f32)

    return {
        "x": jax.random.normal(ks[0], (BATCH, SEQ, D_MODEL), f32),
        "ffn1_norm": gain(ks[1], (DEPTH, D_MODEL)),
        "ffn1_w_gate": nrm(ks[2], (DEPTH, D_MODEL, D_FF), D_MODEL),
        "ffn1_w_up": nrm(ks[3], (DEPTH, D_MODEL, D_FF), D_MODEL),
        "ffn1_w_down": nrm(ks[4], (DEPTH, D_FF, D_MODEL), D_FF),
        "mix_norm": gain(ks[5], (DEPTH, D_MODEL)),
        "w_in": nrm(ks[6], (DEPTH, D_MODEL, IN_WIDTH), D_MODEL),
        "sg_ln_g": gain(ks[7], (DEPTH, SG_WIDTH)),
        "sg_ln_b": 0.02 * jax.random.normal(ks[8], (DEPTH, SG_WIDTH), f32),
        "sg_w": nrm(ks[9], (DEPTH, SG_GROUPS, SG_CHUNK, SG_CHUNK), SG_CHUNK),
        "sg_b": gain(ks[10], (DEPTH, SG_GROUPS, SG_CHUNK)),
        "w_att_out": nrm(ks[11], (DEPTH, ATT_OUT_WIDTH, D_MODEL), ATT_OUT_WIDTH),
        "w_sg_out": nrm(ks[12], (DEPTH, SG_WIDTH, D_MODEL), SG_WIDTH),
        "w_out": nrm(ks[13], (DEPTH, D_MODEL, D_MODEL), D_MODEL),
        "ffn2_norm": gain(ks[14], (DEPTH, D_MODEL)),
        "ffn2_w_gate": nrm(ks[15], (DEPTH, D_MODEL, D_FF), D_MODEL),
        "ffn2_w_up": nrm(ks[16], (DEPTH, D_MODEL, D_FF), D_MODEL),
        "ffn2_w_down": nrm(ks[17], (DEPTH, D_FF, D_MODEL), D_FF),
        "final_norm": gain(ks[18], (D_MODEL,)),
    }


def reference(x, ffn1_norm, ffn1_w_gate, ffn1_w_up, ffn1_w_down, mix_norm, w_in, sg_ln_g, sg_ln_b,
              sg_w, sg_b, w_att_out, w_sg_out, w_out, ffn2_norm, ffn2_w_gate, ffn2_w_up, ffn2_w_down,
              final_norm):
    for i in range(DEPTH):
        x = x + 0.5 * swiglu(rmsnorm(x, ffn1_norm[i]), ffn1_w_gate[i], ffn1_w_up[i], ffn1_w_down[i])
        x = x + hybrid_mixer(rmsnorm(x, mix_norm[i]), w_in[i], sg_ln_g[i], sg_ln_b[i], sg_w[i], sg_b[i],
                             w_att_out[i], w_sg_out[i], w_out[i])
        x = x + 0.5 * swiglu(rmsnorm(x, ffn2_norm[i]), ffn2_w_gate[i], ffn2_w_up[i], ffn2_w_down[i])
    return rmsnorm(x, final_norm)
```

```python
import os
import numpy as np
import ml_dtypes
import concourse.bass as bass
import concourse.mybir as mybir
from concourse.bass_utils import run_bass_kernel_spmd

F32 = mybir.dt.float32
BF16 = mybir.dt.bfloat16
AF = mybir.ActivationFunctionType
ALU = mybir.AluOpType

D = 2048
DFF = 5632
NFF = DFF // 128
KC = D // 128
TOWN = 2048
TALL = 4096
NEG = -30000.0
ENGS = ("pe", "act", "dve", "pool", "sp")
NDMA = 8


class Buf:
    __slots__ = ("name", "w", "r", "excl")

    def __init__(self, name="", excl=False):
        self.name = name
        self.w = None
        self.r = {}
        self.excl = excl


class Sched:
    def __init__(self, nc):
        self.nc = nc
        self.ops = {e: [] for e in ENGS}
        self.sem = {e: nc.alloc_semaphore(name="c_" + e) for e in ENGS}
        self.cnt = {e: 0 for e in ENGS}
        self.seen = {e: {} for e in ENGS}
        self.dq = ("sp", "act", "pool")
        self.dsem = {q: [nc.alloc_semaphore(name="d_%s%d" % (q, i)) for i in range(NDMA)]
                     for q in self.dq}
        self.dcnt = {q: [0] * NDMA for q in self.dq}
        self.dnext = {q: 0 for q in self.dq}

    def _need(self, e, tok, out):
        if tok is None:
            return
        sem, val = tok
        k = sem.name
        if self.seen[e].get(k, 0) >= val:
            return
        if out.get(k, (None, 0))[1] < val:
            out[k] = (sem, val)

    def _gather(self, e, reads, writes):
        need = {}
        for b in reads:
            self._need(e, b.w, need)
            if b.excl:
                for k, t in b.r.items():
                    if k != self.sem[e].name:
                        self._need(e, t, need)
        for b in writes:
            self._need(e, b.w, need)
            for t in b.r.values():
                self._need(e, t, need)
        return need

    def _emit_waits(self, e, need):
        lst = list(need.values())
        for sem, val in lst:
            self.seen[e][sem.name] = val
        if lst:
            self.ops[e].append(("wait", lst))

    def _commit(self, tok, reads, writes):
        k = tok[0].name
        for b in reads:
            if b.r.get(k, (None, 0))[1] < tok[1]:
                b.r[k] = tok
        for b in writes:
            b.w = tok
            b.r = {}

    def op(self, e, fn, reads=(), writes=()):
        need = self._gather(e, reads, writes)
        if e == "pe":
            need.pop(self.sem["pe"].name, None)
        self._emit_waits(e, need)
        self.cnt[e] += 1
        tok = (self.sem[e], self.cnt[e])
        self.ops[e].append(("op", fn, tok))
        self._commit(tok, reads, writes)
        return tok

    def dma(self, q, fn, reads=(), writes=()):
        need = self._gather(q, reads, writes)
        i = self.dnext[q]
        self.dnext[q] = (i + 1) % NDMA
        sem = self.dsem[q][i]
        if self.dcnt[q][i] > 0:
            self._need(q, (sem, self.dcnt[q][i]), need)
        self._emit_waits(q, need)
        self.dcnt[q][i] += 16
        tok = (sem, self.dcnt[q][i])
        self.ops[q].append(("dma", fn, tok))
        self._commit(tok, reads, writes)
        return tok

    def barrier(self):
        toks = [(self.sem[e], self.cnt[e]) for e in ENGS if self.cnt[e] > 0]
        for q in self.dq:
            for i in range(NDMA):
                if self.dcnt[q][i] > 0:
                    toks.append((self.dsem[q][i], self.dcnt[q][i]))
        for e in ENGS:
            need = {}
            for t in toks:
                if e == "pe" and t[0].name == self.sem["pe"].name:
                    continue
                self._need(e, t, need)
            self._emit_waits(e, need)

    def emit(self):
        nc = self.nc
        with nc.Block() as block:
            def run(e, engine):
                for item in self.ops[e]:
                    if item[0] == "wait":
                        for sem, val in item[1]:
                            engine.wait_ge(sem, val)
                    elif item[0] == "op":
                        item[1](engine).then_inc(item[2][0], 1)
                    else:
                        item[1](engine).then_inc(item[2][0], 16)

            @block.tensor
            def _(eng):
                run("pe", eng)

            @block.scalar
            def _(eng):
                run("act", eng)

            @block.vector
            def _(eng):
                run("dve", eng)

            @block.gpsimd
            def _(eng):
                run("pool", eng)

            @block.sync
            def _(eng):
                run("sp", eng)


class Ring:
    def __init__(self, items):
        self.items = items
        self.i = 0

    def next(self):
        it = self.items[self.i]
        self.i = (self.i + 1) % len(self.items)
        return it


def build_program(debug=False, stages=("ffn1", "qkv", "att", "mix", "ffn2", "final"), ffn1_tiles=4):
    nc = bass.Bass("TRN2", target_bir_lowering=False)
    S = Sched(nc)

    def din(name, shape, dt=F32):
        return nc.dram_tensor(name, list(shape), dt, kind="ExternalInput").ap()

    def dscr(name, shape, dt):
        return nc.dram_tensor(name, list(shape), dt, kind="ExternalOutput" if debug else "Internal").ap()

    XT = din("xT", [D, TALL])
    COS = din("cosT", [32, TALL])
    SIN = din("sinT", [32, TALL])
    GAINS = din("gains", [128, 4 * KC])
    WGU = [din("wgu1", [NFF, 128, 2 * KC * 128]), din("wgu2", [NFF, 128, 2 * KC * 128])]
    WD = [din("wd1", [KC, 128, NFF * 128]), din("wd2", [KC, 128, NFF * 128])]
    WIN = din("win", [92, 128, KC * 128])
    WINV = din("winv", [3, 128, KC * 512])
    WINVS = din("winvs", [3, 128, KC * 512])
    WATT = din("watt", [KC, 128, 4 * 128])
    WSG = din("wsg", [KC, 128, 12 * 128])
    WOUT = din("wout", [KC, 128, KC * 128])
    SGWT = din("sgwT", [128, 12 * 128])
    SGB = din("sgb", [1, 12 * 128])
    LNG = din("lng", [128, 1536])
    LNB = din("lnb", [128, 1536])
    CST = din("cst", [128, 1024])
    OUT = nc.dram_tensor("outT", [D, TOWN], F32, kind="ExternalOutput").ap()

    X1 = dscr("x1", [D, TALL], F32)
    X2 = dscr("x2", [D, TOWN], F32)
    X3 = dscr("x3", [D, TOWN], F32)
    H2 = dscr("h2", [D, TOWN], BF16)
    QTd = dscr("qt", [12, 128, TOWN], BF16)
    KTd = dscr("kt", [12, 128, TALL], BF16)
    Vd = dscr("vv", [3, TALL, 512], BF16)
    OTd = dscr("ot", [4, 128, TOWN], BF16)

    ARENA_F = 51200
    arena = nc.alloc_sbuf_tensor("arena", [128, ARENA_F], F32)
    st = {"off": 0}

    def carve(n_elems, dt):
        nbytes = n_elems * (4 if dt == F32 else 2)
        nw = (nbytes + 3) // 4
        nw = (nw + 7) // 8 * 8
        o = st["off"]
        st["off"] = o + nw
        assert st["off"] <= ARENA_F, ("SBUF arena overflow", st["off"])
        a = arena[:, o:o + nw]
        if dt == BF16:
            a = a.bitcast(BF16)
            return a[:, 0:n_elems]
        return a[:, 0:n_elems]

    psr = Ring([(nc.alloc_psum_tensor("ps%d" % i, [128, 512], F32), Buf("ps%d" % i, excl=True)) for i in range(8)])

    cst = carve(1024, F32)
    b_cst = Buf("cst")
    gains = carve(4 * KC, F32)
    ident_b = carve(128, BF16)
    maskCP_b = carve(256, BF16)
    maskP0_b = carve(128, BF16)
    rot_b = carve(128, BF16)
    ones_b = carve(128, BF16)
    b_cb = Buf("cbf")
    S.dma("sp", lambda e: e.dma_start(out=cst, in_=CST), writes=[b_cst])
    S.dma("sp", lambda e: e.dma_start(out=gains, in_=GAINS), writes=[b_cst])
    ident_f = cst[:, 0:128]
    ones_f = cst[:, 768:896]
    caus_f = cst[:, 640:768]
    S.op("dve", lambda e: e.tensor_copy(out=ident_b, in_=cst[:, 0:128]), reads=[b_cst], writes=[b_cb])
    S.op("dve", lambda e: e.tensor_copy(out=maskCP_b, in_=cst[:, 128:384]), reads=[b_cst], writes=[b_cb])
    S.op("dve", lambda e: e.tensor_copy(out=maskP0_b, in_=cst[:, 384:512]), reads=[b_cst], writes=[b_cb])
    S.op("dve", lambda e: e.tensor_copy(out=rot_b, in_=cst[:, 512:640]), reads=[b_cst], writes=[b_cb])
    S.op("dve", lambda e: e.tensor_copy(out=ones_b, in_=cst[:, 768:896]), reads=[b_cst], writes=[b_cb])
    base_off = st["off"]

    def tile_ring(n, elems, dt, name):
        return Ring([(carve(elems, dt), Buf("%s%d" % (name, i))) for i in range(n)])

    def norm_to_h(xsrc, c0, ntok, gidx, hT, hbufs, xring, sqring, rstd_t):
        for half in range(ntok // 512):
            col = c0 + half * 512
            ps, pb = psr.next()
            for kc in range(KC):
                xt, xb = xring.next()
                S.dma("sp", lambda e, xt=xt, kc=kc, col=col: e.dma_start(
                    out=xt, in_=xsrc[kc * 128:(kc + 1) * 128, col:col + 512]), writes=[xb])
                sq, sb = sqring.next()
                S.op("act", lambda e, sq=sq, xt=xt: e.activation(out=sq, in_=xt, func=AF.Square),
                     reads=[xb], writes=[sb])
                S.op("pe", lambda e, ps=ps, sq=sq, kc=kc: e.matmul(
                    ps[:], lhsT=ones_f, rhs=sq, start=(kc == 0), stop=(kc == KC - 1)),
                    reads=[sb, b_cst], writes=[pb])
            rs, rb = rstd_t
            S.op("dve", lambda e, ps=ps: e.tensor_scalar(out=rs, in0=ps[:], scalar1=1.0 / D, scalar2=1e-6,
                                                         op0=ALU.mult, op1=ALU.add), reads=[pb], writes=[rb])
            S.op("act", lambda e: e.activation(out=rs, in_=rs, func=AF.Sqrt), reads=[rb], writes=[rb])
            S.op("dve", lambda e: e.reciprocal(out=rs, in_=rs), reads=[rb], writes=[rb])
            for kc in range(KC):
                xt, xb = xring.next()
                S.dma("sp", lambda e, xt=xt, kc=kc, col=col: e.dma_start(
                    out=xt, in_=xsrc[kc * 128:(kc + 1) * 128, col:col + 512]), writes=[xb])
                dst = hT[:, kc * ntok + half * 512: kc * ntok + half * 512 + 512]
                S.op("dve", lambda e, dst=dst, xt=xt, kc=kc: e.scalar_tensor_tensor(
                    out=dst, in0=xt, scalar=gains[:, gidx * KC + kc: gidx * KC + kc + 1], in1=rs,
                    op0=ALU.mult, op1=ALU.mult), reads=[xb, rb, b_cst], writes=[hbufs[half]])

    def wload(dst, src, wb):
        S.dma("pool", lambda e: e.dma_start(out=dst, in_=src, max_dma_last_dim=8192), writes=[wb])

    def ffn_stages(which, xsrc, xdst, cols):
        st["off"] = base_off
        NT = 1024
        hT = carve(KC * NT, BF16)
        hbufs = [Buf("h0"), Buf("h1")]
        aT = carve(NFF * NT, BF16)
        abufs = [Buf("a0"), Buf("a1")]
        wring = tile_ring(3, NFF * 128, BF16, "w")
        xring = tile_ring(4, 512, F32, "x")
        sqring = tile_ring(2, 512, F32, "sq")
        sgring = tile_ring(2, 512, F32, "sg")
        oring = tile_ring(2, 512, F32, "o")
        rstd_t = (carve(512, F32), Buf("rstd"))
        wgu, wd = WGU[which], WD[which]
        gidx = 0 if which == 0 else 2
        for (c0, oc0) in cols:
            norm_to_h(xsrc, c0, NT, gidx, hT, hbufs, xring, sqring, rstd_t)
            for j in range(NFF):
                wt, wb = wring.next()
                wload(wt[:, 0:2 * KC * 128], wgu[j], wb)
                for t in range(2):
                    pg, pgb = psr.next()
                    pu, pub = psr.next()

                    def mm(e, ps, m, t=t, wt=wt):
                        ins = None
                        for kc in range(KC):
                            ins = e.matmul(ps[:], lhsT=wt[:, (m * KC + kc) * 128:(m * KC + kc + 1) * 128],
                                           rhs=hT[:, kc * NT + t * 512: kc * NT + t * 512 + 512],
                                           start=(kc == 0), stop=(kc == KC - 1))
                        return ins
                    S.op("pe", lambda e, pg=pg, mm=mm: mm(e, pg, 0), reads=[wb, hbufs[t]], writes=[pgb])
                    S.op("pe", lambda e, pu=pu, mm=mm: mm(e, pu, 1), reads=[wb, hbufs[t]], writes=[pub])
                    sg, sgb_ = sgring.next()
                    S.op("act", lambda e, sg=sg, pg=pg: e.activation(out=sg, in_=pg[:], func=AF.Silu),
                         reads=[pgb], writes=[sgb_])
                    dst = aT[:, j * NT + t * 512: j * NT + t * 512 + 512]
                    S.op("dve", lambda e, dst=dst, pu=pu, sg=sg: e.tensor_tensor(
                        out=dst, in0=pu[:], in1=sg, op=ALU.mult), reads=[pub, sgb_], writes=[abufs[t]])
            for i in range(KC):
                wt, wb = wring.next()
                wload(wt[:, 0:NFF * 128], wd[i], wb)
                for t in range(2):
                    xt, xb = xring.next()
                    S.dma("sp", lambda e, xt=xt, i=i, t=t, c0=c0: e.dma_start(
                        out=xt, in_=xsrc[i * 128:(i + 1) * 128, c0 + t * 512: c0 + t * 512 + 512]), writes=[xb])
                    ps, pb = psr.next()

                    def mm(e, ps=ps, t=t, wt=wt):
                        ins = None
                        for jc in range(NFF):
                            ins = e.matmul(ps[:], lhsT=wt[:, jc * 128:(jc + 1) * 128],
                                           rhs=aT[:, jc * NT + t * 512: jc * NT + t * 512 + 512],
                                           start=(jc == 0), stop=(jc == NFF - 1))
                        return ins
                    S.op("pe", mm, reads=[wb, abufs[t]], writes=[pb])
                    ot, ob = oring.next()
                    S.op("dve", lambda e, ot=ot, ps=ps, xt=xt: e.scalar_tensor_tensor(
                        out=ot, in0=ps[:], scalar=0.5, in1=xt, op0=ALU.mult, op1=ALU.add),
                        reads=[pb, xb], writes=[ob])
                    S.dma("act", lambda e, ot=ot, i=i, t=t, oc0=oc0: e.dma_start(
                        out=xdst[i * 128:(i + 1) * 128, oc0 + t * 512: oc0 + t * 512 + 512], in_=ot), reads=[ob])
        S.barrier()

    def qkv_stages():
        st["off"] = base_off
        NT = 1024
        hT = carve(KC * NT, BF16)
        hbufs = [Buf("h0"), Buf("h1")]
        wring = tile_ring(3, KC * 128, BF16, "w")
        wvring = tile_ring(2, KC * 512, BF16, "wv")
        xring = tile_ring(4, 512, F32, "x")
        sqring = tile_ring(2, 512, F32, "sq")
        rstd_t = (carve(512, F32), Buf("rstd"))
        cosb = carve(NT, F32)
        sinb = carve(NT, F32)
        b_cs = Buf("cs")
        rawring = tile_ring(3, 512, BF16, "raw")
        t1ring = tile_ring(2, 512, F32, "t1")
        t2ring = tile_ring(2, 512, F32, "t2")
        vring = tile_ring(3, 512, BF16, "v")
        for sti in range(4):
            c0 = sti * NT
            own = sti >= 2
            norm_to_h(X1, c0, NT, 1, hT, hbufs, xring, sqring, rstd_t)
            S.dma("sp", lambda e, c0=c0: e.dma_start(out=cosb[0:32, :], in_=COS[:, c0:c0 + NT]), writes=[b_cs])
            S.dma("sp", lambda e, c0=c0: e.dma_start(out=sinb[0:32, :], in_=SIN[:, c0:c0 + NT]), writes=[b_cs])
            PARTS = os.environ.get("QKV_PARTS", "h2,qk,rope,st,v").split(",")
            if own and "h2" in PARTS:
                for kc in range(KC):
                    S.dma("sp", lambda e, kc=kc, c0=c0: e.dma_start(
                        out=H2[kc * 128:(kc + 1) * 128, c0 - TOWN: c0 - TOWN + NT],
                        in_=hT[:, kc * NT:(kc + 1) * NT]), reads=hbufs)
            for c in ((range(24) if own else range(12, 24)) if "qk" in PARTS else []):
                wt, wb = wring.next()
                wload(wt, WIN[c], wb)
                for t in range(2):
                    ps, pb = psr.next()

                    def mm(e, ps=ps, t=t, wt=wt):
                        ins = None
                        for kc in range(KC):
                            ins = e.matmul(ps[:], lhsT=wt[:, kc * 128:(kc + 1) * 128],
                                           rhs=hT[:, kc * NT + t * 512: kc * NT + t * 512 + 512],
                                           start=(kc == 0), stop=(kc == KC - 1))
                        return ins
                    S.op("pe", mm, reads=[wb, hbufs[t]], writes=[pb])
                    raw, rb = rawring.next()
                    S.op("act", lambda e, raw=raw, ps=ps: e.activation(out=raw, in_=ps[:], func=AF.Copy),
                         reads=[pb], writes=[rb])
                    if "rope" not in PARTS:
                        continue
                    p2, p2b = psr.next()
                    S.op("pe", lambda e, p2=p2, raw=raw: e.matmul(p2[:], lhsT=rot_b, rhs=raw, start=True, stop=True),
                         reads=[rb, b_cb], writes=[p2b])
                    t1, t1b = t1ring.next()
                    t2, t2b = t2ring.next()
                    S.op("dve", lambda e, t1=t1, ps=ps, t=t: e.tensor_tensor(
                        out=t1[0:32, :], in0=ps[0:32, :], in1=cosb[0:32, t * 512:(t + 1) * 512], op=ALU.mult),
                        reads=[pb, b_cs], writes=[t1b])
                    S.op("dve", lambda e, t2=t2, p2=p2, t=t: e.tensor_tensor(
                        out=t2[0:32, :], in0=p2[0:32, :], in1=sinb[0:32, t * 512:(t + 1) * 512], op=ALU.mult),
                        reads=[p2b, b_cs], writes=[t2b])
                    S.op("dve", lambda e, raw=raw, t1=t1, t2=t2: e.tensor_tensor(
                        out=raw[0:32, :], in0=t1[0:32, :], in1=t2[0:32, :], op=ALU.add),
                        reads=[t1b, t2b], writes=[rb])
                    if c < 12:
                        dst = QTd[c, :, c0 - TOWN + t * 512: c0 - TOWN + t * 512 + 512]
                    else:
                        dst = KTd[c - 12, :, c0 + t * 512: c0 + t * 512 + 512]
                    if "st" in PARTS:
                        S.dma("sp", lambda e, dst=dst, raw=raw: e.dma_start(out=dst, in_=raw), reads=[rb])
            for g in (range(3) if "v" in PARTS else []):
                wv, wvb = wvring.next()
                wload(wv, WINV[g], wvb)
                for tb in range(NT // 128):
                    ps, pb = psr.next()

                    def mm(e, ps=ps, tb=tb, wv=wv):
                        ins = None
                        for kc in range(KC):
                            ins = e.matmul(ps[:], lhsT=hT[:, kc * NT + tb * 128: kc * NT + tb * 128 + 128],
                                           rhs=wv[:, kc * 512:(kc + 1) * 512],
                                           start=(kc == 0), stop=(kc == KC - 1))
                        return ins
                    S.op("pe", mm, reads=[wvb, hbufs[tb // 4]], writes=[pb])
                    vt, vb = vring.next()
                    S.op("act", lambda e, vt=vt, ps=ps: e.activation(out=vt, in_=ps[:], func=AF.Copy),
                         reads=[pb], writes=[vb])
                    S.dma("sp", lambda e, vt=vt, g=g, tb=tb, c0=c0: e.dma_start(
                        out=Vd[g, c0 + tb * 128: c0 + tb * 128 + 128, :], in_=vt), reads=[vb])
        S.barrier()

    def att_stage():
        st["off"] = base_off
        Vr = carve(32 * 512, BF16)
        b_vr = Buf("vr")
        qn_r = tile_ring(2, TOWN, BF16, "qn")
        qr_r = tile_ring(2, TOWN, BF16, "qr")
        kn_r = tile_ring(2, TALL, BF16, "kn")
        kr_r = tile_ring(2, TALL, BF16, "kr")
        numacc = [carve(TOWN, F32) for _ in range(4)]
        denacc = [carve(TOWN, F32) for _ in range(4)]
        b_acc = [Buf("acc%d" % j) for j in range(4)]
        pring = tile_ring(4, 256, BF16, "p")
        obf = carve(TOWN, BF16)
        b_obf = Buf("obf")
        scale = 1.0 / np.sqrt(128.0)
        ps_s = Ring(psr.items[0:4])
        ps_nd = Ring(psr.items[4:8])
        for g, dil in enumerate((1, 4, 16)):
            nkb = TALL // (128 * dil)
            nbq = TOWN // (128 * dil)
            Lq = TOWN // dil
            Lk = TALL // dil
            vsrc = Vd[g].rearrange("(kb i r) c -> r i kb c", r=dil, i=128)
            for r in range(dil):
                S.dma("sp", lambda e, r=r, vsrc=vsrc, nkb=nkb: e.dma_start(
                    out=Vr[:, r * nkb * 512:(r + 1) * nkb * 512].rearrange("p (k c) -> p k c", c=512),
                    in_=vsrc[r]), writes=[b_vr])
            for j in range(4):
                h = 4 * g + j
                qn, qnb = qn_r.next()
                kn, knb = kn_r.next()
                S.dma("sp", lambda e, qn=qn, h=h: e.dma_start(out=qn, in_=QTd[h]), writes=[qnb])
                S.dma("sp", lambda e, kn=kn, h=h: e.dma_start(out=kn, in_=KTd[h]), writes=[knb])
                if dil == 1:
                    qr, qrb, kr, krb = qn, qnb, kn, knb
                else:
                    qr, qrb = qr_r.next()
                    kr, krb = kr_r.next()
                    S.op("pool", lambda e, qr=qr, qn=qn, dil=dil: e.tensor_copy(
                        out=qr.rearrange("p (r i) -> p r i", r=dil),
                        in_=qn.rearrange("p (i r) -> p r i", r=dil)), reads=[qnb], writes=[qrb])
                    S.op("pool", lambda e, kr=kr, kn=kn, dil=dil: e.tensor_copy(
                        out=kr.rearrange("p (r i) -> p r i", r=dil),
                        in_=kn.rearrange("p (i r) -> p r i", r=dil)), reads=[knb], writes=[krb])
                qblocks = [(r, m) for r in range(dil) for m in range(nbq)]
                for q0 in range(0, len(qblocks), 4):
                    pn, pnb = ps_nd.next()
                    pd, pdb = ps_nd.next()
                    for s4 in range(4):
                        r, m = qblocks[q0 + s4]
                        qcol = r * Lq + m * 128
                        parts = []
                        for role, kb in (("prev", nbq + m - 1), ("cur", nbq + m)):
                            ps, pb = ps_s.next()
                            kcol = r * Lk + kb * 128
                            if role == "cur":
                                mk = maskCP_b[:, 0:128]
                            elif m == 0:
                                mk = maskP0_b
                            else:
                                mk = maskCP_b[:, 128:256]

                            def mms(e, ps=ps, kcol=kcol, qcol=qcol, mk=mk, kr=kr, qr=qr):
                                e.matmul(ps[:, 0:128], lhsT=kr[:, kcol:kcol + 128], rhs=qr[:, qcol:qcol + 128],
                                         start=True, stop=False)
                                return e.matmul(ps[:, 0:128], lhsT=ident_b, rhs=mk, start=False, stop=True)
                            S.op("pe", mms, reads=[krb, qrb, b_cb], writes=[pb])
                            pt, ptb = pring.next()
                            S.op("act", lambda e, pt=pt, ps=ps: e.activation(
                                out=pt[:, 0:128], in_=ps[:, 0:128], func=AF.Exp, scale=float(scale)),
                                reads=[pb], writes=[ptb])
                            parts.append((pt, ptb, r * nkb + kb))

                        def pv(e, pn=pn, pd=pd, s4=s4, parts=parts, j=j):
                            ins = None
                            for idx, (pt, ptb, blk) in enumerate(parts):
                                e.matmul(pn[:, s4 * 128:(s4 + 1) * 128],
                                         lhsT=Vr[:, blk * 512 + j * 128: blk * 512 + (j + 1) * 128],
                                         rhs=pt[:, 0:128], start=(idx == 0), stop=(idx == 1))
                            for idx, (pt, ptb, blk) in enumerate(parts):
                                ins = e.matmul(pd[:, s4 * 128:(s4 + 1) * 128], lhsT=ones_b, rhs=pt[:, 0:128],
                                               start=(idx == 0), stop=(idx == 1))
                            return ins
                        S.op("pe", pv, reads=[parts[0][1], parts[1][1], b_vr, b_cb], writes=[pnb, pdb])
                    r0, m0 = qblocks[q0]
                    if dil == 1:
                        tok = slice(m0 * 128, m0 * 128 + 512)
                        na, da = numacc[j][:, tok], denacc[j][:, tok]
                        pna, pda = pn[:], pd[:]
                    elif dil == 4:
                        na = numacc[j].rearrange("p (i r) -> p r i", r=4)[:, r0, :]
                        da = denacc[j].rearrange("p (i r) -> p r i", r=4)[:, r0, :]
                        pna, pda = pn[:], pd[:]
                    else:
                        na = numacc[j].rearrange("p (i r) -> p r i", r=16)[:, r0:r0 + 4, :]
                        da = denacc[j].rearrange("p (i r) -> p r i", r=16)[:, r0:r0 + 4, :]
                        pna = pn[:].rearrange("p (s i) -> p s i", s=4)
                        pda = pd[:].rearrange("p (s i) -> p s i", s=4)
                    if g == 0:
                        S.op("dve", lambda e, na=na, pna=pna: e.tensor_copy(out=na, in_=pna),
                             reads=[pnb], writes=[b_acc[j]])
                        S.op("dve", lambda e, da=da, pda=pda: e.tensor_copy(out=da, in_=pda),
                             reads=[pdb], writes=[b_acc[j]])
                    else:
                        S.op("dve", lambda e, na=na, pna=pna: e.tensor_tensor(out=na, in0=pna, in1=na, op=ALU.add),
                             reads=[pnb], writes=[b_acc[j]])
                        S.op("dve", lambda e, da=da, pda=pda: e.tensor_tensor(out=da, in0=pda, in1=da, op=ALU.add),
                             reads=[pdb], writes=[b_acc[j]])
        for j in range(4):
            S.op("dve", lambda e, j=j: e.reciprocal(out=denacc[j], in_=denacc[j]), writes=[b_acc[j]])
            S.op("dve", lambda e, j=j: e.tensor_tensor(out=obf, in0=numacc[j], in1=denacc[j], op=ALU.mult),
                 reads=[b_acc[j]], writes=[b_obf])
            S.dma("sp", lambda e, j=j: e.dma_start(out=OTd[j], in_=obf), reads=[b_obf])
        S.barrier()

    def mix_stages():
        st["off"] = base_off
        NT = 512
        h2 = carve(KC * NT, BF16)
        b_h2 = Buf("h2")
        oT = carve(4 * NT, BF16)
        b_oT = Buf("oT")
        uT = carve(12 * NT, F32)
        b_uT = Buf("uT")
        vsn = [carve(1536, BF16) for _ in range(4)]
        b_vsn = [Buf("vsn%d" % i) for i in range(4)]
        sgT = carve(12 * NT, BF16)
        b_sgT = Buf("sgT")
        lng = carve(1536, F32)
        lnb = carve(1536, F32)
        wsp = carve(12 * 128, BF16)
        sgb = carve(12 * 128, F32)
        b_c2 = Buf("c2")
        stats = carve(18, F32)
        mv = carve(2, F32)
        rstd = carve(1, F32)
        b_st = Buf("st")
        wring = tile_ring(4, KC * 128, BF16, "w")
        tmpw = carve(12 * 128, F32)
        S.dma("sp", lambda e: e.dma_start(out=lng, in_=LNG), writes=[b_c2])
        S.dma("sp", lambda e: e.dma_start(out=lnb, in_=LNB), writes=[b_c2])
        S.dma("sp", lambda e: e.dma_start(out=sgb[0:1, :], in_=SGB), writes=[b_c2])
        S.dma("sp", lambda e: e.dma_start(out=tmpw, in_=SGWT), writes=[b_c2])
        for gq in range(12):
            S.op("dve", lambda e, gq=gq: e.tensor_tensor(
                out=wsp[:, gq * 128:(gq + 1) * 128], in0=tmpw[:, gq * 128:(gq + 1) * 128],
                in1=caus_f, op=ALU.mult), reads=[b_c2, b_cst], writes=[b_c2])
        alias0 = st["off"]
        for sti in range(TOWN // NT):
            oc0 = sti * NT
            st["off"] = alias0
            wvs = [carve(KC * 512, BF16) for _ in range(3)]
            b_wvs = [Buf("wvs%d" % i) for i in range(3)]
            gv = carve(1536, F32)
            b_gv = Buf("gv")
            for kc in range(KC):
                S.dma("sp", lambda e, kc=kc, oc0=oc0: e.dma_start(
                    out=h2[:, kc * NT:(kc + 1) * NT], in_=H2[kc * 128:(kc + 1) * 128, oc0:oc0 + NT]), writes=[b_h2])
            for j in range(4):
                S.dma("sp", lambda e, j=j, oc0=oc0: e.dma_start(
                    out=oT[:, j * NT:(j + 1) * NT], in_=OTd[j, :, oc0:oc0 + NT]), writes=[b_oT])
            for ct in range(3):
                wload(wvs[ct], WINVS[ct], b_wvs[ct])
            for c in range(12):
                wt, wb = wring.next()
                wload(wt, WIN[36 + c], wb)
                ps, pb = psr.next()

                def mm(e, ps=ps, wt=wt):
                    ins = None
                    for kc in range(KC):
                        ins = e.matmul(ps[:], lhsT=wt[:, kc * 128:(kc + 1) * 128], rhs=h2[:, kc * NT:(kc + 1) * NT],
                                       start=(kc == 0), stop=(kc == KC - 1))
                    return ins
                S.op("pe", mm, reads=[wb, b_h2], writes=[pb])
                S.op("act", lambda e, c=c, ps=ps: e.activation(out=uT[:, c * NT:(c + 1) * NT], in_=ps[:], func=AF.Gelu),
                     reads=[pb], writes=[b_uT])
            for tb in range(4):
                for ct in range(3):
                    ps, pb = psr.next()

                    def mm(e, ps=ps, ct=ct, tb=tb):
                        ins = None
                        for kc in range(KC):
                            ins = e.matmul(ps[:], lhsT=h2[:, kc * NT + tb * 128: kc * NT + tb * 128 + 128],
                                           rhs=wvs[ct][:, kc * 512:(kc + 1) * 512],
                                           start=(kc == 0), stop=(kc == KC - 1))
                        return ins
                    S.op("pe", mm, reads=[b_wvs[ct], b_h2], writes=[pb])
                    S.op("act", lambda e, ps=ps, ct=ct: e.activation(
                        out=gv[:, ct * 512:(ct + 1) * 512], in_=ps[:], func=AF.Gelu), reads=[pb], writes=[b_gv])
                for ct in range(3):
                    S.op("dve", lambda e, ct=ct: e.bn_stats(out=stats[:, ct * 6:(ct + 1) * 6],
                                                            in_=gv[:, ct * 512:(ct + 1) * 512]),
                         reads=[b_gv], writes=[b_st])
                S.op("dve", lambda e: e.bn_aggr(out=mv, in_=stats), reads=[b_st], writes=[b_st])
                S.op("dve", lambda e: e.tensor_scalar(out=rstd, in0=mv[:, 1:2], scalar1=1e-5, scalar2=1.0,
                                                      op0=ALU.add, op1=ALU.mult), reads=[b_st], writes=[b_st])
                S.op("act", lambda e: e.activation(out=rstd, in_=rstd, func=AF.Sqrt), reads=[b_st], writes=[b_st])
                S.op("dve", lambda e: e.reciprocal(out=rstd, in_=rstd), reads=[b_st], writes=[b_st])
                S.op("dve", lambda e: e.tensor_scalar(out=gv, in0=gv, scalar1=mv[:, 0:1], scalar2=rstd[:, 0:1],
                                                      op0=ALU.subtract, op1=ALU.mult), reads=[b_st, b_gv], writes=[b_gv])
                S.op("dve", lambda e: e.tensor_tensor(out=gv, in0=gv, in1=lng, op=ALU.mult),
                     reads=[b_gv, b_c2], writes=[b_gv])
                S.op("dve", lambda e, tb=tb: e.tensor_tensor(out=vsn[tb], in0=gv, in1=lnb, op=ALU.add),
                     reads=[b_gv, b_c2], writes=[b_vsn[tb]])
            for grp in range(12):
                ps, pb = psr.next()

                def mm(e, ps=ps, grp=grp):
                    ins = None
                    for tb in range(4):
                        e.matmul(ps[:, tb * 128:(tb + 1) * 128], lhsT=ones_f[0:1, :],
                                 rhs=sgb[0:1, grp * 128:(grp + 1) * 128], start=True, stop=False)
                        ins = e.matmul(ps[:, tb * 128:(tb + 1) * 128], lhsT=vsn[tb][:, grp * 128:(grp + 1) * 128],
                                       rhs=wsp[:, grp * 128:(grp + 1) * 128], start=False, stop=True)
                    return ins
                S.op("pe", mm, reads=b_vsn + [b_c2, b_cst], writes=[pb])
                S.op("dve", lambda e, ps=ps, grp=grp: e.tensor_tensor(
                    out=sgT[:, grp * NT:(grp + 1) * NT], in0=ps[:], in1=uT[:, grp * NT:(grp + 1) * NT], op=ALU.mult),
                    reads=[pb, b_uT], writes=[b_sgT])
            S.barrier()
            st["off"] = alias0
            merged = carve(KC * NT, BF16)
            b_mg = Buf("mg")
            sa_r = tile_ring(2, NT, F32, "sa")
            ss_r = tile_ring(2, NT, F32, "ss")
            m1_r = tile_ring(2, NT, F32, "m1")
            m2_r = tile_ring(2, NT, F32, "m2")
            x_r = tile_ring(3, NT, F32, "xr")
            o_r = tile_ring(2, NT, F32, "or")
            w2ring = tile_ring(2, 12 * 128, BF16, "w2")
            w3ring = tile_ring(2, 4 * 128, BF16, "w3")
            for i in range(KC):
                wa, wab = wring.next()
                wload(wa, WIN[60 + i], wab)
                ws, wsb = wring.next()
                wload(ws, WIN[76 + i], wsb)
                w2, w2b = w2ring.next()
                wload(w2, WSG[i], w2b)
                w3, w3b = w3ring.next()
                wload(w3, WATT[i], w3b)
                pga, pgab = psr.next()
                pgs, pgsb = psr.next()
                pya, pyab = psr.next()
                pys, pysb = psr.next()

                def mmg(e, ps, wt):
                    ins = None
                    for kc in range(KC):
                        ins = e.matmul(ps[:], lhsT=wt[:, kc * 128:(kc + 1) * 128], rhs=h2[:, kc * NT:(kc + 1) * NT],
                                       start=(kc == 0), stop=(kc == KC - 1))
                    return ins
                S.op("pe", lambda e, pga=pga, wa=wa, mmg=mmg: mmg(e, pga, wa), reads=[wab, b_h2], writes=[pgab])
                S.op("pe", lambda e, pgs=pgs, ws=ws, mmg=mmg: mmg(e, pgs, ws), reads=[wsb, b_h2], writes=[pgsb])

                def mmya(e, pya=pya, w3=w3):
                    ins = None
                    for j in range(4):
                        ins = e.matmul(pya[:], lhsT=w3[:, j * 128:(j + 1) * 128], rhs=oT[:, j * NT:(j + 1) * NT],
                                       start=(j == 0), stop=(j == 3))
                    return ins
                S.op("pe", mmya, reads=[w3b, b_oT], writes=[pyab])

                def mmys(e, pys=pys, w2=w2):
                    ins = None
                    for gq in range(12):
                        ins = e.matmul(pys[:], lhsT=w2[:, gq * 128:(gq + 1) * 128], rhs=sgT[:, gq * NT:(gq + 1) * NT],
                                       start=(gq == 0), stop=(gq == 11))
                    return ins
                S.op("pe", mmys, reads=[w2b, b_sgT], writes=[pysb])
                sa, sab = sa_r.next()
                ss, ssb = ss_r.next()
                S.op("act", lambda e, sa=sa, pga=pga: e.activation(out=sa, in_=pga[:], func=AF.Sigmoid),
                     reads=[pgab], writes=[sab])
                S.op("act", lambda e, ss=ss, pgs=pgs: e.activation(out=ss, in_=pgs[:], func=AF.Sigmoid),
                     reads=[pgsb], writes=[ssb])
                m1, m1b = m1_r.next()
                m2, m2b = m2_r.next()
                S.op("dve", lambda e, m1=m1, pya=pya, sa=sa: e.tensor_tensor(out=m1, in0=pya[:], in1=sa, op=ALU.mult),
                     reads=[pyab, sab], writes=[m1b])
                S.op("dve", lambda e, m2=m2, pys=pys, ss=ss: e.tensor_tensor(out=m2, in0=pys[:], in1=ss, op=ALU.mult),
                     reads=[pysb, ssb], writes=[m2b])
                S.op("pool", lambda e, i=i, m1=m1, m2=m2: e.tensor_tensor(
                    out=merged[:, i * NT:(i + 1) * NT], in0=m1, in1=m2, op=ALU.add),
                    reads=[m1b, m2b], writes=[b_mg])
            for i in range(KC):
                wo, wob = wring.next()
                wload(wo, WOUT[i], wob)
                xt, xb = x_r.next()
                S.dma("sp", lambda e, xt=xt, i=i, oc0=oc0: e.dma_start(
                    out=xt, in_=X1[i * 128:(i + 1) * 128, TOWN + oc0: TOWN + oc0 + NT]), writes=[xb])
                ps, pb = psr.next()

                def mm(e, ps=ps, wo=wo):
                    ins = None
                    for kc in range(KC):
                        ins = e.matmul(ps[:], lhsT=wo[:, kc * 128:(kc + 1) * 128], rhs=merged[:, kc * NT:(kc + 1) * NT],
                                       start=(kc == 0), stop=(kc == KC - 1))
                    return ins
                S.op("pe", mm, reads=[wob, b_mg], writes=[pb])
                ot, ob = o_r.next()
                S.op("dve", lambda e, ot=ot, ps=ps, xt=xt: e.tensor_tensor(out=ot, in0=ps[:], in1=xt, op=ALU.add),
                     reads=[pb, xb], writes=[ob])
                S.dma("act", lambda e, ot=ot, i=i, oc0=oc0: e.dma_start(
                    out=X2[i * 128:(i + 1) * 128, oc0:oc0 + NT], in_=ot), reads=[ob])
            S.barrier()

    def final_stage():
        st["off"] = base_off
        xring = tile_ring(4, 512, F32, "x")
        sqring = tile_ring(2, 512, F32, "sq")
        oring = tile_ring(3, 512, F32, "o")
        rs = carve(512, F32)
        rb = Buf("rstd")
        b_out = Buf("out")
        for tt in range(TOWN // 512):
            col = tt * 512
            ps, pb = psr.next()
            for kc in range(KC):
                xt, xb = xring.next()
                S.dma("sp", lambda e, xt=xt, kc=kc, col=col: e.dma_start(
                    out=xt, in_=X3[kc * 128:(kc + 1) * 128, col:col + 512]), writes=[xb])
                sq, sb = sqring.next()
                S.op("act", lambda e, sq=sq, xt=xt: e.activation(out=sq, in_=xt, func=AF.Square),
                     reads=[xb], writes=[sb])
                S.op("pe", lambda e, ps=ps, sq=sq, kc=kc: e.matmul(
                    ps[:], lhsT=ones_f, rhs=sq, start=(kc == 0), stop=(kc == KC - 1)),
                    reads=[sb, b_cst], writes=[pb])
            S.op("dve", lambda e, ps=ps: e.tensor_scalar(out=rs, in0=ps[:], scalar1=1.0 / D, scalar2=1e-6,
                                                         op0=ALU.mult, op1=ALU.add), reads=[pb], writes=[rb])
            S.op("act", lambda e: e.activation(out=rs, in_=rs, func=AF.Sqrt), reads=[rb], writes=[rb])
            S.op("dve", lambda e: e.reciprocal(out=rs, in_=rs), reads=[rb], writes=[rb])
            for kc in range(KC):
                xt, xb = xring.next()
                S.dma("sp", lambda e, xt=xt, kc=kc, col=col: e.dma_start(
                    out=xt, in_=X3[kc * 128:(kc + 1) * 128, col:col + 512]), writes=[xb])
                ot, ob = oring.next()
                S.op("dve", lambda e, ot=ot, xt=xt, kc=kc: e.scalar_tensor_tensor(
                    out=ot, in0=xt, scalar=gains[:, 3 * KC + kc: 3 * KC + kc + 1], in1=rs,
                    op0=ALU.mult, op1=ALU.mult), reads=[xb, rb, b_cst], writes=[ob])
                S.dma("act", lambda e, ot=ot, kc=kc, col=col: e.dma_start(
                    out=OUT[kc * 128:(kc + 1) * 128, col:col + 512], in_=ot), reads=[ob], writes=[b_out])
        S.barrier()

    if "ffn1" in stages:
        ffn_stages(0, XT, X1, [(0, 0), (1024, 1024), (2048, 2048), (3072, 3072)][4 - ffn1_tiles:])
    if "qkv" in stages:
        qkv_stages()
    if "att" in stages:
        att_stage()
    if "mix" in stages:
        mix_stages()
    if "ffn2" in stages:
        ffn_stages(1, X2, X3, [(0, 0), (1024, 1024)])
    if "final" in stages:
        final_stage()
    S.barrier()
    S.emit()
    return nc


_PROG = {}


def _consts():
    c = np.zeros((128, 1024), np.float32)
    p = np.arange(128)[:, None]
    f = np.arange(128)[None, :]
    c[:, 0:128] = np.eye(128, dtype=np.float32)
    c[:, 128:256] = np.where(p <= f, 0.0, NEG)
    c[:, 256:384] = np.where(p >= f, 0.0, NEG)
    rot = np.zeros((32, 32), np.float32)
    for i in range(16):
        rot[i + 16, i] = -1.0
        rot[i, i + 16] = 1.0
    c[0:32, 512:544] = rot
    c[:, 640:768] = (p <= f).astype(np.float32)
    c[:, 768:896] = 1.0
    return c


def _prep_shared(inp):
    f32 = np.float32

    def tile_w(w, ncol):
        K, N = w.shape
        return np.ascontiguousarray(
            w.reshape(K // 128, 128, N // ncol, ncol).transpose(2, 1, 0, 3).reshape(N // ncol, 128, (K // 128) * ncol))

    sh = {}
    for n, (wg, wu, wd) in enumerate((("ffn1_w_gate", "ffn1_w_up", "ffn1_w_down"),
                                      ("ffn2_w_gate", "ffn2_w_up", "ffn2_w_down"))):
        g = tile_w(inp[wg][0], 128)
        u = tile_w(inp[wu][0], 128)
        sh["wgu%d" % (n + 1)] = np.ascontiguousarray(np.concatenate([g, u], axis=2))
        sh["wd%d" % (n + 1)] = tile_w(inp[wd][0], 128)
    w_in = inp["w_in"][0]
    sh["win"] = tile_w(w_in, 128)
    sh["winv"] = tile_w(w_in[:, 3072:4608], 512)
    sh["winvs"] = tile_w(w_in[:, 6144:7680], 512)
    sh["watt"] = tile_w(inp["w_att_out"][0], 128)
    sh["wsg"] = tile_w(inp["w_sg_out"][0], 128)
    sh["wout"] = tile_w(inp["w_out"][0], 128)
    sgw = inp["sg_w"][0]
    sh["sgwT"] = np.ascontiguousarray(sgw.transpose(2, 0, 1).reshape(128, 12 * 128))
    sh["sgb"] = np.ascontiguousarray(inp["sg_b"][0].reshape(1, 12 * 128))
    sh["lng"] = np.ascontiguousarray(np.broadcast_to(inp["sg_ln_g"][0][None, :], (128, 1536))).astype(f32)
    sh["lnb"] = np.ascontiguousarray(np.broadcast_to(inp["sg_ln_b"][0][None, :], (128, 1536))).astype(f32)
    gains = np.stack([inp["ffn1_norm"][0], inp["mix_norm"][0], inp["ffn2_norm"][0], inp["final_norm"]])
    sh["gains"] = np.ascontiguousarray(gains.reshape(4, KC, 128).transpose(2, 0, 1).reshape(128, 4 * KC))
    return sh


def _rope_tables(pos):
    inv_freq = (np.float32(500000.0) ** (-np.arange(0, 32, 2, dtype=np.float32) / np.float32(32))).astype(np.float32)
    ang = (pos.astype(np.float32)[None, :] * inv_freq[:, None]).astype(np.float32)
    cos = np.cos(ang.astype(np.float64)).astype(np.float32)
    sin = np.sin(ang.astype(np.float64)).astype(np.float32)
    return np.concatenate([cos, cos], 0), np.concatenate([sin, sin], 0)


def make_in_maps(inp):
    x = inp["x"]
    sh = _prep_shared(inp)
    cbase = _consts()
    in_maps = []
    for c in range(8):
        b, half = c // 2, c % 2
        own = x[b, half * TOWN:(half + 1) * TOWN]
        prev = x[b, 0:TOWN]
        xT = np.ascontiguousarray(np.concatenate([prev.T, own.T], axis=1))
        if half == 1:
            pos = np.arange(0, TALL)
        else:
            pos = np.concatenate([np.arange(0, TOWN), np.arange(0, TOWN)])
        cosT, sinT = _rope_tables(pos)
        cst = cbase.copy()
        cst[:, 384:512] = cbase[:, 256:384] if half == 1 else NEG
        m = dict(sh)
        m.update({"xT": xT, "cosT": np.ascontiguousarray(cosT), "sinT": np.ascontiguousarray(sinT), "cst": cst})
        in_maps.append(m)
    return in_maps


def kernel(**inputs):
    inp = {k: np.asarray(v) for k, v in inputs.items()}
    if "nc" not in _PROG:
        _PROG["nc"] = build_program()
    nc = _PROG["nc"]
    in_maps = make_in_maps(inp)
    res = run_bass_kernel_spmd(nc, in_maps, core_ids=list(range(8)))
    out = np.empty((4, 4096, D), np.float32)
    for c in range(8):
        b, half = c // 2, c % 2
        out[b, half * TOWN:(half + 1) * TOWN] = res.results[c]["outT"].T
    return out
```

```python
import os
import numpy as np
import ml_dtypes
import concourse.bass as bass
import concourse.mybir as mybir
from concourse.bass_utils import run_bass_kernel_spmd

F32 = mybir.dt.float32
BF16 = mybir.dt.bfloat16
AF = mybir.ActivationFunctionType
ALU = mybir.AluOpType

D = 2048
DFF = 5632
NFF = DFF // 128
KC = D // 128
TOWN = 2048
TALL = 4096
NEG = -30000.0
ENGS = ("pe", "act", "dve", "pool", "sp")
NDMA = 8


class Buf:
    __slots__ = ("name", "w", "r", "excl")

    def __init__(self, name="", excl=False):
        self.name = name
        self.w = None
        self.r = {}
        self.excl = excl


class Sched:
    def __init__(self, nc):
        self.nc = nc
        self.ops = {e: [] for e in ENGS}
        self.sem = {e: nc.alloc_semaphore(name="c_" + e) for e in ENGS}
        self.cnt = {e: 0 for e in ENGS}
        self.seen = {e: {} for e in ENGS}
        self.dq = ("sp", "act", "pool")
        self.dsem = {q: [nc.alloc_semaphore(name="d_%s%d" % (q, i)) for i in range(NDMA)]
                     for q in self.dq}
        self.dcnt = {q: [0] * NDMA for q in self.dq}
        self.dnext = {q: 0 for q in self.dq}

    def _need(self, e, tok, out):
        if tok is None:
            return
        sem, val = tok
        k = sem.name
        if self.seen[e].get(k, 0) >= val:
            return
        if out.get(k, (None, 0))[1] < val:
            out[k] = (sem, val)

    def _gather(self, e, reads, writes):
        need = {}
        for b in reads:
            self._need(e, b.w, need)
            if b.excl:
                for k, t in b.r.items():
                    if k != self.sem[e].name:
                        self._need(e, t, need)
        for b in writes:
            self._need(e, b.w, need)
            for t in b.r.values():
                self._need(e, t, need)
        return need

    def _emit_waits(self, e, need):
        lst = list(need.values())
        for sem, val in lst:
            self.seen[e][sem.name] = val
        if lst:
            self.ops[e].append(("wait", lst))

    def _commit(self, tok, reads, writes):
        k = tok[0].name
        for b in reads:
            if b.r.get(k, (None, 0))[1] < tok[1]:
                b.r[k] = tok
        for b in writes:
            b.w = tok
            b.r = {}

    def op(self, e, fn, reads=(), writes=()):
        need = self._gather(e, reads, writes)
        if e == "pe":
            need.pop(self.sem["pe"].name, None)
        self._emit_waits(e, need)
        self.cnt[e] += 1
        tok = (self.sem[e], self.cnt[e])
        self.ops[e].append(("op", fn, tok))
        self._commit(tok, reads, writes)
        return tok

    def dma(self, q, fn, reads=(), writes=()):
        need = self._gather(q, reads, writes)
        i = self.dnext[q]
        self.dnext[q] = (i + 1) % NDMA
        sem = self.dsem[q][i]
        if self.dcnt[q][i] > 0:
            self._need(q, (sem, self.dcnt[q][i]), need)
        self._emit_waits(q, need)
        self.dcnt[q][i] += 16
        tok = (sem, self.dcnt[q][i])
        self.ops[q].append(("dma", fn, tok))
        self._commit(tok, reads, writes)
        return tok

    def barrier(self):
        toks = [(self.sem[e], self.cnt[e]) for e in ENGS if self.cnt[e] > 0]
        for q in self.dq:
            for i in range(NDMA):
                if self.dcnt[q][i] > 0:
                    toks.append((self.dsem[q][i], self.dcnt[q][i]))
        for e in ENGS:
            need = {}
            for t in toks:
                if e == "pe" and t[0].name == self.sem["pe"].name:
                    continue
                self._need(e, t, need)
            self._emit_waits(e, need)

    def emit(self):
        nc = self.nc
        with nc.Block() as block:
            def run(e, engine):
                for item in self.ops[e]:
                    if item[0] == "wait":
                        for sem, val in item[1]:
                            engine.wait_ge(sem, val)
                    elif item[0] == "op":
                        item[1](engine).then_inc(item[2][0], 1)
                    else:
                        item[1](engine).then_inc(item[2][0], 16)

            @block.tensor
            def _(eng):
                run("pe", eng)

            @block.scalar
            def _(eng):
                run("act", eng)

            @block.vector
            def _(eng):
                run("dve", eng)

            @block.gpsimd
            def _(eng):
                run("pool", eng)

            @block.sync
            def _(eng):
                run("sp", eng)


class Ring:
    def __init__(self, items):
        self.items = items
        self.i = 0

    def next(self):
        it = self.items[self.i]
        self.i = (self.i + 1) % len(self.items)
        return it


def build_program(debug=False, stages=("ffn1", "qkv", "att", "mix", "ffn2", "final"), ffn1_tiles=4):
    nc = bass.Bass("TRN2", target_bir_lowering=False)
    S = Sched(nc)

    def din(name, shape, dt=F32):
        return nc.dram_tensor(name, list(shape), dt, kind="ExternalInput").ap()

    def dscr(name, shape, dt):
        return nc.dram_tensor(name, list(shape), dt, kind="ExternalOutput" if debug else "Internal").ap()

    XT = din("xT", [D, TALL])
    COS = din("cosT", [32, TALL])
    SIN = din("sinT", [32, TALL])
    GAINS = din("gains", [128, 4 * KC])
    WGU = [din("wgu1", [NFF, 128, 2 * KC * 128]), din("wgu2", [NFF, 128, 2 * KC * 128])]
    WD = [din("wd1", [KC, 128, NFF * 128]), din("wd2", [KC, 128, NFF * 128])]
    WIN = din("win", [92, 128, KC * 128])
    WINV = din("winv", [3, 128, KC * 512])
    WINVS = din("winvs", [3, 128, KC * 512])
    WATT = din("watt", [KC, 128, 4 * 128])
    WSG = din("wsg", [KC, 128, 12 * 128])
    WOUT = din("wout", [KC, 128, KC * 128])
    SGWT = din("sgwT", [128, 12 * 128])
    SGB = din("sgb", [1, 12 * 128])
    LNG = din("lng", [128, 1536])
    LNB = din("lnb", [128, 1536])
    CST = din("cst", [128, 1024])
    OUT = nc.dram_tensor("outT", [D, TOWN], F32, kind="ExternalOutput").ap()

    X1 = dscr("x1", [D, TALL], F32)
    X2 = dscr("x2", [D, TOWN], F32)
    X3 = dscr("x3", [D, TOWN], F32)
    H2 = dscr("h2", [D, TOWN], BF16)
    QTd = dscr("qt", [12, 128, TOWN], BF16)
    KTd = dscr("kt", [12, 128, TALL], BF16)
    Vd = dscr("vv", [3, TALL, 512], BF16)
    OTd = dscr("ot", [4, 128, TOWN], BF16)

    ARENA_F = 51200
    arena = nc.alloc_sbuf_tensor("arena", [128, ARENA_F], F32)
    st = {"off": 0}

    def carve(n_elems, dt):
        nbytes = n_elems * (4 if dt == F32 else 2)
        nw = (nbytes + 3) // 4
        nw = (nw + 7) // 8 * 8
        o = st["off"]
        st["off"] = o + nw
        assert st["off"] <= ARENA_F, ("SBUF arena overflow", st["off"])
        a = arena[:, o:o + nw]
        if dt == BF16:
            a = a.bitcast(BF16)
            return a[:, 0:n_elems]
        return a[:, 0:n_elems]

    psr = Ring([(nc.alloc_psum_tensor("ps%d" % i, [128, 512], F32), Buf("ps%d" % i, excl=True)) for i in range(8)])

    cst = carve(1024, F32)
    b_cst = Buf("cst")
    gains = carve(4 * KC, F32)
    ident_b = carve(128, BF16)
    maskCP_b = carve(256, BF16)
    maskP0_b = carve(128, BF16)
    rot_b = carve(128, BF16)
    ones_b = carve(128, BF16)
    b_cb = Buf("cbf")
    S.dma("sp", lambda e: e.dma_start(out=cst, in_=CST), writes=[b_cst])
    S.dma("sp", lambda e: e.dma_start(out=gains, in_=GAINS), writes=[b_cst])
    ident_f = cst[:, 0:128]
    ones_f = cst[:, 768:896]
    caus_f = cst[:, 640:768]
    S.op("dve", lambda e: e.tensor_copy(out=ident_b, in_=cst[:, 0:128]), reads=[b_cst], writes=[b_cb])
    S.op("dve", lambda e: e.tensor_copy(out=maskCP_b, in_=cst[:, 128:384]), reads=[b_cst], writes=[b_cb])
    S.op("dve", lambda e: e.tensor_copy(out=maskP0_b, in_=cst[:, 384:512]), reads=[b_cst], writes=[b_cb])
    S.op("dve", lambda e: e.tensor_copy(out=rot_b, in_=cst[:, 512:640]), reads=[b_cst], writes=[b_cb])
    S.op("dve", lambda e: e.tensor_copy(out=ones_b, in_=cst[:, 768:896]), reads=[b_cst], writes=[b_cb])
    base_off = st["off"]

    def tile_ring(n, elems, dt, name):
        return Ring([(carve(elems, dt), Buf("%s%d" % (name, i))) for i in range(n)])

    def norm_to_h(xsrc, c0, ntok, gidx, hT, hbufs, xring, sqring, rstd_t):
        for half in range(ntok // 512):
            col = c0 + half * 512
            ps, pb = psr.next()
            for kc in range(KC):
                xt, xb = xring.next()
                S.dma("sp", lambda e, xt=xt, kc=kc, col=col: e.dma_start(
                    out=xt, in_=xsrc[kc * 128:(kc + 1) * 128, col:col + 512]), writes=[xb])
                sq, sb = sqring.next()
                S.op("act", lambda e, sq=sq, xt=xt: e.activation(out=sq, in_=xt, func=AF.Square),
                     reads=[xb], writes=[sb])
                S.op("pe", lambda e, ps=ps, sq=sq, kc=kc: e.matmul(
                    ps[:], lhsT=ones_f, rhs=sq, start=(kc == 0), stop=(kc == KC - 1)),
                    reads=[sb, b_cst], writes=[pb])
            rs, rb = rstd_t
            S.op("dve", lambda e, ps=ps: e.tensor_scalar(out=rs, in0=ps[:], scalar1=1.0 / D, scalar2=1e-6,
                                                         op0=ALU.mult, op1=ALU.add), reads=[pb], writes=[rb])
            S.op("act", lambda e: e.activation(out=rs, in_=rs, func=AF.Sqrt), reads=[rb], writes=[rb])
            S.op("dve", lambda e: e.reciprocal(out=rs, in_=rs), reads=[rb], writes=[rb])
            for kc in range(KC):
                xt, xb = xring.next()
                S.dma("sp", lambda e, xt=xt, kc=kc, col=col: e.dma_start(
                    out=xt, in_=xsrc[kc * 128:(kc + 1) * 128, col:col + 512]), writes=[xb])
                dst = hT[:, kc * ntok + half * 512: kc * ntok + half * 512 + 512]
                S.op("dve", lambda e, dst=dst, xt=xt, kc=kc: e.scalar_tensor_tensor(
                    out=dst, in0=xt, scalar=gains[:, gidx * KC + kc: gidx * KC + kc + 1], in1=rs,
                    op0=ALU.mult, op1=ALU.mult), reads=[xb, rb, b_cst], writes=[hbufs[half]])

    def wload(dst, src, wb):
        S.dma("pool", lambda e: e.dma_start(out=dst, in_=src, max_dma_last_dim=8192), writes=[wb])

    def ffn_stages(which, xsrc, xdst, cols):
        st["off"] = base_off
        NT = 1024
        hT = carve(KC * NT, BF16)
        hbufs = [Buf("h0"), Buf("h1")]
        aT = carve(NFF * NT, BF16)
        abufs = [Buf("a0"), Buf("a1")]
        wring = tile_ring(3, NFF * 128, BF16, "w")
        xring = tile_ring(4, 512, F32, "x")
        sqring = tile_ring(2, 512, F32, "sq")
        sgring = tile_ring(2, 512, F32, "sg")
        oring = tile_ring(2, 512, F32, "o")
        rstd_t = (carve(512, F32), Buf("rstd"))
        wgu, wd = WGU[which], WD[which]
        gidx = 0 if which == 0 else 2
        for (c0, oc0) in cols:
            norm_to_h(xsrc, c0, NT, gidx, hT, hbufs, xring, sqring, rstd_t)
            for j in range(NFF):
                wt, wb = wring.next()
                wload(wt[:, 0:2 * KC * 128], wgu[j], wb)
                for t in range(2):
                    pg, pgb = psr.next()
                    pu, pub = psr.next()

                    def mm(e, ps, m, t=t, wt=wt):
                        ins = None
                        for kc in range(KC):
                            ins = e.matmul(ps[:], lhsT=wt[:, (m * KC + kc) * 128:(m * KC + kc + 1) * 128],
                                           rhs=hT[:, kc * NT + t * 512: kc * NT + t * 512 + 512],
                                           start=(kc == 0), stop=(kc == KC - 1))
                        return ins
                    S.op("pe", lambda e, pg=pg, mm=mm: mm(e, pg, 0), reads=[wb, hbufs[t]], writes=[pgb])
                    S.op("pe", lambda e, pu=pu, mm=mm: mm(e, pu, 1), reads=[wb, hbufs[t]], writes=[pub])
                    sg, sgb_ = sgring.next()
                    S.op("act", lambda e, sg=sg, pg=pg: e.activation(out=sg, in_=pg[:], func=AF.Silu),
                         reads=[pgb], writes=[sgb_])
                    dst = aT[:, j * NT + t * 512: j * NT + t * 512 + 512]
                    S.op("dve", lambda e, dst=dst, pu=pu, sg=sg: e.tensor_tensor(
                        out=dst, in0=pu[:], in1=sg, op=ALU.mult), reads=[pub, sgb_], writes=[abufs[t]])
            for i in range(KC):
                wt, wb = wring.next()
                wload(wt[:, 0:NFF * 128], wd[i], wb)
                for t in range(2):
                    xt, xb = xring.next()
                    S.dma("sp", lambda e, xt=xt, i=i, t=t, c0=c0: e.dma_start(
                        out=xt, in_=xsrc[i * 128:(i + 1) * 128, c0 + t * 512: c0 + t * 512 + 512]), writes=[xb])
                    ps, pb = psr.next()

                    def mm(e, ps=ps, t=t, wt=wt):
                        ins = None
                        for jc in range(NFF):
                            ins = e.matmul(ps[:], lhsT=wt[:, jc * 128:(jc + 1) * 128],
                                           rhs=aT[:, jc * NT + t * 512: jc * NT + t * 512 + 512],
                                           start=(jc == 0), stop=(jc == NFF - 1))
                        return ins
                    S.op("pe", mm, reads=[wb, abufs[t]], writes=[pb])
                    ot, ob = oring.next()
                    S.op("dve", lambda e, ot=ot, ps=ps, xt=xt: e.scalar_tensor_tensor(
                        out=ot, in0=ps[:], scalar=0.5, in1=xt, op0=ALU.mult, op1=ALU.add),
                        reads=[pb, xb], writes=[ob])
                    S.dma("act", lambda e, ot=ot, i=i, t=t, oc0=oc0: e.dma_start(
                        out=xdst[i * 128:(i + 1) * 128, oc0 + t * 512: oc0 + t * 512 + 512], in_=ot), reads=[ob])
        S.barrier()

    def qkv_stages():
        st["off"] = base_off
        NT = 1024
        hT = carve(KC * NT, BF16)
        hbufs = [Buf("h0"), Buf("h1")]
        wring = tile_ring(3, KC * 128, BF16, "w")
        wvring = tile_ring(2, KC * 512, BF16, "wv")
        xring = tile_ring(4, 512, F32, "x")
        sqring = tile_ring(2, 512, F32, "sq")
        rstd_t = (carve(512, F32), Buf("rstd"))
        cosb = carve(NT, F32)
        sinb = carve(NT, F32)
        b_cs = Buf("cs")
        rawring = tile_ring(3, 512, BF16, "raw")
        t1ring = tile_ring(2, 512, F32, "t1")
        t2ring = tile_ring(2, 512, F32, "t2")
        vring = tile_ring(3, 512, BF16, "v")
        for sti in range(4):
            c0 = sti * NT
            own = sti >= 2
            norm_to_h(X1, c0, NT, 1, hT, hbufs, xring, sqring, rstd_t)
            S.dma("sp", lambda e, c0=c0: e.dma_start(out=cosb[0:32, :], in_=COS[:, c0:c0 + NT]), writes=[b_cs])
            S.dma("sp", lambda e, c0=c0: e.dma_start(out=sinb[0:32, :], in_=SIN[:, c0:c0 + NT]), writes=[b_cs])
            PARTS = os.environ.get("QKV_PARTS", "h2,qk,rope,st,v").split(",")
            if own and "h2" in PARTS:
                for kc in range(KC):
                    S.dma("sp", lambda e, kc=kc, c0=c0: e.dma_start(
                        out=H2[kc * 128:(kc + 1) * 128, c0 - TOWN: c0 - TOWN + NT],
                        in_=hT[:, kc * NT:(kc + 1) * NT]), reads=hbufs)
            for c in ((range(24) if own else range(12, 24)) if "qk" in PARTS else []):
                if own or c >= 20:
                    tlist = (0, 1)
                elif sti == 1:
                    tlist = (1,)
                else:
                    continue
                wt, wb = wring.next()
                wload(wt, WIN[c], wb)
                for t in tlist:
                    ps, pb = psr.next()

                    def mm(e, ps=ps, t=t, wt=wt):
                        ins = None
                        for kc in range(KC):
                            ins = e.matmul(ps[:], lhsT=wt[:, kc * 128:(kc + 1) * 128],
                                           rhs=hT[:, kc * NT + t * 512: kc * NT + t * 512 + 512],
                                           start=(kc == 0), stop=(kc == KC - 1))
                        return ins
                    S.op("pe", mm, reads=[wb, hbufs[t]], writes=[pb])
                    raw, rb = rawring.next()
                    S.op("act", lambda e, raw=raw, ps=ps: e.activation(out=raw, in_=ps[:], func=AF.Copy),
                         reads=[pb], writes=[rb])
                    if "rope" not in PARTS:
                        continue
                    p2, p2b = psr.next()
                    S.op("pe", lambda e, p2=p2, raw=raw: e.matmul(p2[:], lhsT=rot_b, rhs=raw, start=True, stop=True),
                         reads=[rb, b_cb], writes=[p2b])
                    t1, t1b = t1ring.next()
                    t2, t2b = t2ring.next()
                    S.op("dve", lambda e, t1=t1, ps=ps, t=t: e.tensor_tensor(
                        out=t1[0:32, :], in0=ps[0:32, :], in1=cosb[0:32, t * 512:(t + 1) * 512], op=ALU.mult),
                        reads=[pb, b_cs], writes=[t1b])
                    S.op("dve", lambda e, t2=t2, p2=p2, t=t: e.tensor_tensor(
                        out=t2[0:32, :], in0=p2[0:32, :], in1=sinb[0:32, t * 512:(t + 1) * 512], op=ALU.mult),
                        reads=[p2b, b_cs], writes=[t2b])
                    S.op("dve", lambda e, raw=raw, t1=t1, t2=t2: e.tensor_tensor(
                        out=raw[0:32, :], in0=t1[0:32, :], in1=t2[0:32, :], op=ALU.add),
                        reads=[t1b, t2b], writes=[rb])
                    if c < 12:
                        dst = QTd[c, :, c0 - TOWN + t * 512: c0 - TOWN + t * 512 + 512]
                    else:
                        dst = KTd[c - 12, :, c0 + t * 512: c0 + t * 512 + 512]
                    if "st" in PARTS:
                        S.dma("sp", lambda e, dst=dst, raw=raw: e.dma_start(out=dst, in_=raw), reads=[rb])
            for g in (range(3) if "v" in PARTS else []):
                if own or g == 2:
                    tbs = range(NT // 128)
                elif sti == 1:
                    tbs = range(4, 8)
                else:
                    continue
                wv, wvb = wvring.next()
                wload(wv, WINV[g], wvb)
                for tb in tbs:
                    ps, pb = psr.next()

                    def mm(e, ps=ps, tb=tb, wv=wv):
                        ins = None
                        for kc in range(KC):
                            ins = e.matmul(ps[:], lhsT=hT[:, kc * NT + tb * 128: kc * NT + tb * 128 + 128],
                                           rhs=wv[:, kc * 512:(kc + 1) * 512],
                                           start=(kc == 0), stop=(kc == KC - 1))
                        return ins
                    S.op("pe", mm, reads=[wvb, hbufs[tb // 4]], writes=[pb])
                    vt, vb = vring.next()
                    S.op("act", lambda e, vt=vt, ps=ps: e.activation(out=vt, in_=ps[:], func=AF.Copy),
                         reads=[pb], writes=[vb])
                    S.dma("sp", lambda e, vt=vt, g=g, tb=tb, c0=c0: e.dma_start(
                        out=Vd[g, c0 + tb * 128: c0 + tb * 128 + 128, :], in_=vt), reads=[vb])
        S.barrier()

    def att_stage():
        st["off"] = base_off
        Vr = carve(32 * 512, BF16)
        b_vr = Buf("vr")
        qn_r = tile_ring(2, TOWN, BF16, "qn")
        qr_r = tile_ring(2, TOWN, BF16, "qr")
        kn_r = tile_ring(2, TALL, BF16, "kn")
        kr_r = tile_ring(2, TALL, BF16, "kr")
        numacc = [carve(TOWN, F32) for _ in range(4)]
        denacc = [carve(TOWN, F32) for _ in range(4)]
        b_acc = [Buf("acc%d" % j) for j in range(4)]
        pring = tile_ring(4, 256, BF16, "p")
        obf = carve(TOWN, BF16)
        b_obf = Buf("obf")
        scale = 1.0 / np.sqrt(128.0)
        ps_s = Ring(psr.items[0:4])
        ps_nd = Ring(psr.items[4:8])
        for g, dil in enumerate((1, 4, 16)):
            nkb = TALL // (128 * dil)
            nbq = TOWN // (128 * dil)
            Lq = TOWN // dil
            Lk = TALL // dil
            vsrc = Vd[g].rearrange("(kb i r) c -> r i kb c", r=dil, i=128)
            for r in range(dil):
                S.dma("sp", lambda e, r=r, vsrc=vsrc, nkb=nkb: e.dma_start(
                    out=Vr[:, r * nkb * 512:(r + 1) * nkb * 512].rearrange("p (k c) -> p k c", c=512),
                    in_=vsrc[r]), writes=[b_vr])
            for j in range(4):
                h = 4 * g + j
                qn, qnb = qn_r.next()
                kn, knb = kn_r.next()
                S.dma("sp", lambda e, qn=qn, h=h: e.dma_start(out=qn, in_=QTd[h]), writes=[qnb])
                S.dma("sp", lambda e, kn=kn, h=h: e.dma_start(out=kn, in_=KTd[h]), writes=[knb])
                if dil == 1:
                    qr, qrb, kr, krb = qn, qnb, kn, knb
                else:
                    qr, qrb = qr_r.next()
                    kr, krb = kr_r.next()
                    S.op("pool", lambda e, qr=qr, qn=qn, dil=dil: e.tensor_copy(
                        out=qr.rearrange("p (r i) -> p r i", r=dil),
                        in_=qn.rearrange("p (i r) -> p r i", r=dil)), reads=[qnb], writes=[qrb])
                    S.op("pool", lambda e, kr=kr, kn=kn, dil=dil: e.tensor_copy(
                        out=kr.rearrange("p (r i) -> p r i", r=dil),
                        in_=kn.rearrange("p (i r) -> p r i", r=dil)), reads=[knb], writes=[krb])
                qblocks = [(r, m) for r in range(dil) for m in range(nbq)]
                def issue_scores(qi):
                        r, m = qblocks[qi]
                        qcol = r * Lq + m * 128
                        parts = []
                        for role, kb in (("prev", nbq + m - 1), ("cur", nbq + m)):
                            ps, pb = ps_s.next()
                            kcol = r * Lk + kb * 128
                            if role == "cur":
                                mk = maskCP_b[:, 0:128]
                            elif m == 0:
                                mk = maskP0_b
                            else:
                                mk = maskCP_b[:, 128:256]

                            def mms(e, ps=ps, kcol=kcol, qcol=qcol, mk=mk, kr=kr, qr=qr):
                                e.matmul(ps[:, 0:128], lhsT=kr[:, kcol:kcol + 128], rhs=qr[:, qcol:qcol + 128],
                                         start=True, stop=False)
                                return e.matmul(ps[:, 0:128], lhsT=ident_b, rhs=mk, start=False, stop=True)
                            S.op("pe", mms, reads=[krb, qrb, b_cb], writes=[pb])
                            pt, ptb = pring.next()
                            S.op("act", lambda e, pt=pt, ps=ps: e.activation(
                                out=pt[:, 0:128], in_=ps[:, 0:128], func=AF.Exp, scale=float(scale)),
                                reads=[pb], writes=[ptb])
                            parts.append((pt, ptb, r * nkb + kb))
                        return parts

                nxt = issue_scores(0)
                for q0 in range(0, len(qblocks), 4):
                    pn, pnb = ps_nd.next()
                    pd, pdb = ps_nd.next()
                    for s4 in range(4):
                        parts = nxt
                        if q0 + s4 + 1 < len(qblocks):
                            nxt = issue_scores(q0 + s4 + 1)

                        def pv(e, pn=pn, pd=pd, s4=s4, parts=parts, j=j):
                            ins = None
                            for idx, (pt, ptb, blk) in enumerate(parts):
                                e.matmul(pn[:, s4 * 128:(s4 + 1) * 128],
                                         lhsT=Vr[:, blk * 512 + j * 128: blk * 512 + (j + 1) * 128],
                                         rhs=pt[:, 0:128], start=(idx == 0), stop=(idx == 1))
                            for idx, (pt, ptb, blk) in enumerate(parts):
                                ins = e.matmul(pd[:, s4 * 128:(s4 + 1) * 128], lhsT=ones_b, rhs=pt[:, 0:128],
                                               start=(idx == 0), stop=(idx == 1))
                            return ins
                        S.op("pe", pv, reads=[parts[0][1], parts[1][1], b_vr, b_cb], writes=[pnb, pdb])
                    r0, m0 = qblocks[q0]
                    if dil == 1:
                        tok = slice(m0 * 128, m0 * 128 + 512)
                        na, da = numacc[j][:, tok], denacc[j][:, tok]
                        pna, pda = pn[:], pd[:]
                    elif dil == 4:
                        na = numacc[j].rearrange("p (i r) -> p r i", r=4)[:, r0, :]
                        da = denacc[j].rearrange("p (i r) -> p r i", r=4)[:, r0, :]
                        pna, pda = pn[:], pd[:]
                    else:
                        na = numacc[j].rearrange("p (i r) -> p r i", r=16)[:, r0:r0 + 4, :]
                        da = denacc[j].rearrange("p (i r) -> p r i", r=16)[:, r0:r0 + 4, :]
                        pna = pn[:].rearrange("p (s i) -> p s i", s=4)
                        pda = pd[:].rearrange("p (s i) -> p s i", s=4)
                    if g == 0:
                        S.op("dve", lambda e, na=na, pna=pna: e.tensor_copy(out=na, in_=pna),
                             reads=[pnb], writes=[b_acc[j]])
                        S.op("dve", lambda e, da=da, pda=pda: e.tensor_copy(out=da, in_=pda),
                             reads=[pdb], writes=[b_acc[j]])
                    else:
                        S.op("dve", lambda e, na=na, pna=pna: e.tensor_tensor(out=na, in0=pna, in1=na, op=ALU.add),
                             reads=[pnb], writes=[b_acc[j]])
                        S.op("dve", lambda e, da=da, pda=pda: e.tensor_tensor(out=da, in0=pda, in1=da, op=ALU.add),
                             reads=[pdb], writes=[b_acc[j]])
        for j in range(4):
            S.op("dve", lambda e, j=j: e.reciprocal(out=denacc[j], in_=denacc[j]), writes=[b_acc[j]])
            S.op("dve", lambda e, j=j: e.tensor_tensor(out=obf, in0=numacc[j], in1=denacc[j], op=ALU.mult),
                 reads=[b_acc[j]], writes=[b_obf])
            S.dma("sp", lambda e, j=j: e.dma_start(out=OTd[j], in_=obf), reads=[b_obf])
        S.barrier()

    def mix_stages():
        st["off"] = base_off
        NT = 512
        h2 = carve(KC * NT, BF16)
        b_h2 = Buf("h2")
        oT = carve(4 * NT, BF16)
        b_oT = Buf("oT")
        uT = carve(12 * NT, F32)
        b_uT = Buf("uT")
        vsn = [carve(1536, BF16) for _ in range(4)]
        b_vsn = [Buf("vsn%d" % i) for i in range(4)]
        sgT = carve(12 * NT, BF16)
        b_sgT = Buf("sgT")
        lng = carve(1536, F32)
        lnb = carve(1536, F32)
        wsp = carve(12 * 128, BF16)
        sgb = carve(12 * 128, F32)
        b_c2 = Buf("c2")
        stats = carve(18, F32)
        mv = carve(2, F32)
        rstd = carve(1, F32)
        b_st = Buf("st")
        wring = tile_ring(4, KC * 128, BF16, "w")
        tmpw = carve(12 * 128, F32)
        S.dma("sp", lambda e: e.dma_start(out=lng, in_=LNG), writes=[b_c2])
        S.dma("sp", lambda e: e.dma_start(out=lnb, in_=LNB), writes=[b_c2])
        S.dma("sp", lambda e: e.dma_start(out=sgb[0:1, :], in_=SGB), writes=[b_c2])
        S.dma("sp", lambda e: e.dma_start(out=tmpw, in_=SGWT), writes=[b_c2])
        for gq in range(12):
            S.op("dve", lambda e, gq=gq: e.tensor_tensor(
                out=wsp[:, gq * 128:(gq + 1) * 128], in0=tmpw[:, gq * 128:(gq + 1) * 128],
                in1=caus_f, op=ALU.mult), reads=[b_c2, b_cst], writes=[b_c2])
        alias0 = st["off"]
        for sti in range(TOWN // NT):
            oc0 = sti * NT
            st["off"] = alias0
            wvs = [carve(KC * 512, BF16) for _ in range(3)]
            b_wvs = [Buf("wvs%d" % i) for i in range(3)]
            gv = carve(1536, F32)
            b_gv = Buf("gv")
            for kc in range(KC):
                S.dma("sp", lambda e, kc=kc, oc0=oc0: e.dma_start(
                    out=h2[:, kc * NT:(kc + 1) * NT], in_=H2[kc * 128:(kc + 1) * 128, oc0:oc0 + NT]), writes=[b_h2])
            for j in range(4):
                S.dma("sp", lambda e, j=j, oc0=oc0: e.dma_start(
                    out=oT[:, j * NT:(j + 1) * NT], in_=OTd[j, :, oc0:oc0 + NT]), writes=[b_oT])
            for ct in range(3):
                wload(wvs[ct], WINVS[ct], b_wvs[ct])
            for c in range(12):
                wt, wb = wring.next()
                wload(wt, WIN[36 + c], wb)
                ps, pb = psr.next()

                def mm(e, ps=ps, wt=wt):
                    ins = None
                    for kc in range(KC):
                        ins = e.matmul(ps[:], lhsT=wt[:, kc * 128:(kc + 1) * 128], rhs=h2[:, kc * NT:(kc + 1) * NT],
                                       start=(kc == 0), stop=(kc == KC - 1))
                    return ins
                S.op("pe", mm, reads=[wb, b_h2], writes=[pb])
                S.op("act", lambda e, c=c, ps=ps: e.activation(out=uT[:, c * NT:(c + 1) * NT], in_=ps[:], func=AF.Gelu),
                     reads=[pb], writes=[b_uT])
            for tb in range(4):
                for ct in range(3):
                    ps, pb = psr.next()

                    def mm(e, ps=ps, ct=ct, tb=tb):
                        ins = None
                        for kc in range(KC):
                            ins = e.matmul(ps[:], lhsT=h2[:, kc * NT + tb * 128: kc * NT + tb * 128 + 128],
                                           rhs=wvs[ct][:, kc * 512:(kc + 1) * 512],
                                           start=(kc == 0), stop=(kc == KC - 1))
                        return ins
                    S.op("pe", mm, reads=[b_wvs[ct], b_h2], writes=[pb])
                    S.op("act", lambda e, ps=ps, ct=ct: e.activation(
                        out=gv[:, ct * 512:(ct + 1) * 512], in_=ps[:], func=AF.Gelu), reads=[pb], writes=[b_gv])
                for ct in range(3):
                    S.op("dve", lambda e, ct=ct: e.bn_stats(out=stats[:, ct * 6:(ct + 1) * 6],
                                                            in_=gv[:, ct * 512:(ct + 1) * 512]),
                         reads=[b_gv], writes=[b_st])
                S.op("dve", lambda e: e.bn_aggr(out=mv, in_=stats), reads=[b_st], writes=[b_st])
                S.op("dve", lambda e: e.tensor_scalar(out=rstd, in0=mv[:, 1:2], scalar1=1e-5, scalar2=1.0,
                                                      op0=ALU.add, op1=ALU.mult), reads=[b_st], writes=[b_st])
                S.op("act", lambda e: e.activation(out=rstd, in_=rstd, func=AF.Sqrt), reads=[b_st], writes=[b_st])
                S.op("dve", lambda e: e.reciprocal(out=rstd, in_=rstd), reads=[b_st], writes=[b_st])
                S.op("dve", lambda e: e.tensor_scalar(out=gv, in0=gv, scalar1=mv[:, 0:1], scalar2=rstd[:, 0:1],
                                                      op0=ALU.subtract, op1=ALU.mult), reads=[b_st, b_gv], writes=[b_gv])
                S.op("dve", lambda e: e.tensor_tensor(out=gv, in0=gv, in1=lng, op=ALU.mult),
                     reads=[b_gv, b_c2], writes=[b_gv])
                S.op("dve", lambda e, tb=tb: e.tensor_tensor(out=vsn[tb], in0=gv, in1=lnb, op=ALU.add),
                     reads=[b_gv, b_c2], writes=[b_vsn[tb]])
            for grp in range(12):
                ps, pb = psr.next()

                def mm(e, ps=ps, grp=grp):
                    ins = None
                    for tb in range(4):
                        e.matmul(ps[:, tb * 128:(tb + 1) * 128], lhsT=ones_f[0:1, :],
                                 rhs=sgb[0:1, grp * 128:(grp + 1) * 128], start=True, stop=False)
                        ins = e.matmul(ps[:, tb * 128:(tb + 1) * 128], lhsT=vsn[tb][:, grp * 128:(grp + 1) * 128],
                                       rhs=wsp[:, grp * 128:(grp + 1) * 128], start=False, stop=True)
                    return ins
                S.op("pe", mm, reads=b_vsn + [b_c2, b_cst], writes=[pb])
                S.op("dve", lambda e, ps=ps, grp=grp: e.tensor_tensor(
                    out=sgT[:, grp * NT:(grp + 1) * NT], in0=ps[:], in1=uT[:, grp * NT:(grp + 1) * NT], op=ALU.mult),
                    reads=[pb, b_uT], writes=[b_sgT])
            S.barrier()
            st["off"] = alias0
            merged = carve(KC * NT, BF16)
            b_mg = Buf("mg")
            sa_r = tile_ring(2, NT, F32, "sa")
            ss_r = tile_ring(2, NT, F32, "ss")
            m1_r = tile_ring(2, NT, F32, "m1")
            m2_r = tile_ring(2, NT, F32, "m2")
            x_r = tile_ring(3, NT, F32, "xr")
            o_r = tile_ring(2, NT, F32, "or")
            w2ring = tile_ring(2, 12 * 128, BF16, "w2")
            w3ring = tile_ring(2, 4 * 128, BF16, "w3")
            for i in range(KC):
                wa, wab = wring.next()
                wload(wa, WIN[60 + i], wab)
                ws, wsb = wring.next()
                wload(ws, WIN[76 + i], wsb)
                w2, w2b = w2ring.next()
                wload(w2, WSG[i], w2b)
                w3, w3b = w3ring.next()
                wload(w3, WATT[i], w3b)
                pga, pgab = psr.next()
                pgs, pgsb = psr.next()
                pya, pyab = psr.next()
                pys, pysb = psr.next()

                def mmg(e, ps, wt):
                    ins = None
                    for kc in range(KC):
                        ins = e.matmul(ps[:], lhsT=wt[:, kc * 128:(kc + 1) * 128], rhs=h2[:, kc * NT:(kc + 1) * NT],
                                       start=(kc == 0), stop=(kc == KC - 1))
                    return ins
                S.op("pe", lambda e, pga=pga, wa=wa, mmg=mmg: mmg(e, pga, wa), reads=[wab, b_h2], writes=[pgab])
                S.op("pe", lambda e, pgs=pgs, ws=ws, mmg=mmg: mmg(e, pgs, ws), reads=[wsb, b_h2], writes=[pgsb])

                def mmya(e, pya=pya, w3=w3):
                    ins = None
                    for j in range(4):
                        ins = e.matmul(pya[:], lhsT=w3[:, j * 128:(j + 1) * 128], rhs=oT[:, j * NT:(j + 1) * NT],
                                       start=(j == 0), stop=(j == 3))
                    return ins
                S.op("pe", mmya, reads=[w3b, b_oT], writes=[pyab])

                def mmys(e, pys=pys, w2=w2):
                    ins = None
                    for gq in range(12):
                        ins = e.matmul(pys[:], lhsT=w2[:, gq * 128:(gq + 1) * 128], rhs=sgT[:, gq * NT:(gq + 1) * NT],
                                       start=(gq == 0), stop=(gq == 11))
                    return ins
                S.op("pe", mmys, reads=[w2b, b_sgT], writes=[pysb])
                sa, sab = sa_r.next()
                ss, ssb = ss_r.next()
                S.op("act", lambda e, sa=sa, pga=pga: e.activation(out=sa, in_=pga[:], func=AF.Sigmoid),
                     reads=[pgab], writes=[sab])
                S.op("act", lambda e, ss=ss, pgs=pgs: e.activation(out=ss, in_=pgs[:], func=AF.Sigmoid),
                     reads=[pgsb], writes=[ssb])
                m1, m1b = m1_r.next()
                m2, m2b = m2_r.next()
                S.op("dve", lambda e, m1=m1, pya=pya, sa=sa: e.tensor_tensor(out=m1, in0=pya[:], in1=sa, op=ALU.mult),
                     reads=[pyab, sab], writes=[m1b])
                S.op("dve", lambda e, m2=m2, pys=pys, ss=ss: e.tensor_tensor(out=m2, in0=pys[:], in1=ss, op=ALU.mult),
                     reads=[pysb, ssb], writes=[m2b])
                S.op("pool", lambda e, i=i, m1=m1, m2=m2: e.tensor_tensor(
                    out=merged[:, i * NT:(i + 1) * NT], in0=m1, in1=m2, op=ALU.add),
                    reads=[m1b, m2b], writes=[b_mg])
            for i in range(KC):
                wo, wob = wring.next()
                wload(wo, WOUT[i], wob)
                xt, xb = x_r.next()
                S.dma("sp", lambda e, xt=xt, i=i, oc0=oc0: e.dma_start(
                    out=xt, in_=X1[i * 128:(i + 1) * 128, TOWN + oc0: TOWN + oc0 + NT]), writes=[xb])
                ps, pb = psr.next()

                def mm(e, ps=ps, wo=wo):
                    ins = None
                    for kc in range(KC):
                        ins = e.matmul(ps[:], lhsT=wo[:, kc * 128:(kc + 1) * 128], rhs=merged[:, kc * NT:(kc + 1) * NT],
                                       start=(kc == 0), stop=(kc == KC - 1))
                    return ins
                S.op("pe", mm, reads=[wob, b_mg], writes=[pb])
                ot, ob = o_r.next()
                S.op("dve", lambda e, ot=ot, ps=ps, xt=xt: e.tensor_tensor(out=ot, in0=ps[:], in1=xt, op=ALU.add),
                     reads=[pb, xb], writes=[ob])
                S.dma("act", lambda e, ot=ot, i=i, oc0=oc0: e.dma_start(
                    out=X2[i * 128:(i + 1) * 128, oc0:oc0 + NT], in_=ot), reads=[ob])
            S.barrier()

    def final_stage():
        st["off"] = base_off
        xring = tile_ring(4, 512, F32, "x")
        sqring = tile_ring(2, 512, F32, "sq")
        oring = tile_ring(3, 512, F32, "o")
        rs = carve(512, F32)
        rb = Buf("rstd")
        b_out = Buf("out")
        for tt in range(TOWN // 512):
            col = tt * 512
            ps, pb = psr.next()
            for kc in range(KC):
                xt, xb = xring.next()
                S.dma("sp", lambda e, xt=xt, kc=kc, col=col: e.dma_start(
                    out=xt, in_=X3[kc * 128:(kc + 1) * 128, col:col + 512]), writes=[xb])
                sq, sb = sqring.next()
                S.op("act", lambda e, sq=sq, xt=xt: e.activation(out=sq, in_=xt, func=AF.Square),
                     reads=[xb], writes=[sb])
                S.op("pe", lambda e, ps=ps, sq=sq, kc=kc: e.matmul(
                    ps[:], lhsT=ones_f, rhs=sq, start=(kc == 0), stop=(kc == KC - 1)),
                    reads=[sb, b_cst], writes=[pb])
            S.op("dve", lambda e, ps=ps: e.tensor_scalar(out=rs, in0=ps[:], scalar1=1.0 / D, scalar2=1e-6,
                                                         op0=ALU.mult, op1=ALU.add), reads=[pb], writes=[rb])
            S.op("act", lambda e: e.activation(out=rs, in_=rs, func=AF.Sqrt), reads=[rb], writes=[rb])
            S.op("dve", lambda e: e.reciprocal(out=rs, in_=rs), reads=[rb], writes=[rb])
            for kc in range(KC):
                xt, xb = xring.next()
                S.dma("sp", lambda e, xt=xt, kc=kc, col=col: e.dma_start(
                    out=xt, in_=X3[kc * 128:(kc + 1) * 128, col:col + 512]), writes=[xb])
                ot, ob = oring.next()
                S.op("dve", lambda e, ot=ot, xt=xt, kc=kc: e.scalar_tensor_tensor(
                    out=ot, in0=xt, scalar=gains[:, 3 * KC + kc: 3 * KC + kc + 1], in1=rs,
                    op0=ALU.mult, op1=ALU.mult), reads=[xb, rb, b_cst], writes=[ob])
                S.dma("act", lambda e, ot=ot, kc=kc, col=col: e.dma_start(
                    out=OUT[kc * 128:(kc + 1) * 128, col:col + 512], in_=ot), reads=[ob], writes=[b_out])
        S.barrier()

    if "ffn1" in stages:
        ffn_stages(0, XT, X1, [(0, 0), (1024, 1024), (2048, 2048), (3072, 3072)][4 - ffn1_tiles:])
    if "qkv" in stages:
        qkv_stages()
    if "att" in stages:
        att_stage()
    if "mix" in stages:
        mix_stages()
    if "ffn2" in stages:
        ffn_stages(1, X2, X3, [(0, 0), (1024, 1024)])
    if "final" in stages:
        final_stage()
    S.barrier()
    S.emit()
    return nc


_PROG = {}


def _consts():
    c = np.zeros((128, 1024), np.float32)
    p = np.arange(128)[:, None]
    f = np.arange(128)[None, :]
    c[:, 0:128] = np.eye(128, dtype=np.float32)
    c[:, 128:256] = np.where(p <= f, 0.0, NEG)
    c[:, 256:384] = np.where(p >= f, 0.0, NEG)
    rot = np.zeros((32, 32), np.float32)
    for i in range(16):
        rot[i + 16, i] = -1.0
        rot[i, i + 16] = 1.0
    c[0:32, 512:544] = rot
    c[:, 640:768] = (p <= f).astype(np.float32)
    c[:, 768:896] = 1.0
    return c


def _prep_shared(inp):
    f32 = np.float32

    def tile_w(w, ncol):
        K, N = w.shape
        return np.ascontiguousarray(
            w.reshape(K // 128, 128, N // ncol, ncol).transpose(2, 1, 0, 3).reshape(N // ncol, 128, (K // 128) * ncol))

    sh = {}
    for n, (wg, wu, wd) in enumerate((("ffn1_w_gate", "ffn1_w_up", "ffn1_w_down"),
                                      ("ffn2_w_gate", "ffn2_w_up", "ffn2_w_down"))):
        g = tile_w(inp[wg][0], 128)
        u = tile_w(inp[wu][0], 128)
        sh["wgu%d" % (n + 1)] = np.ascontiguousarray(np.concatenate([g, u], axis=2))
        sh["wd%d" % (n + 1)] = tile_w(inp[wd][0], 128)
    w_in = inp["w_in"][0]
    sh["win"] = tile_w(w_in, 128)
    sh["winv"] = tile_w(w_in[:, 3072:4608], 512)
    sh["winvs"] = tile_w(w_in[:, 6144:7680], 512)
    sh["watt"] = tile_w(inp["w_att_out"][0], 128)
    sh["wsg"] = tile_w(inp["w_sg_out"][0], 128)
    sh["wout"] = tile_w(inp["w_out"][0], 128)
    sgw = inp["sg_w"][0]
    sh["sgwT"] = np.ascontiguousarray(sgw.transpose(2, 0, 1).reshape(128, 12 * 128))
    sh["sgb"] = np.ascontiguousarray(inp["sg_b"][0].reshape(1, 12 * 128))
    sh["lng"] = np.ascontiguousarray(np.broadcast_to(inp["sg_ln_g"][0][None, :], (128, 1536))).astype(f32)
    sh["lnb"] = np.ascontiguousarray(np.broadcast_to(inp["sg_ln_b"][0][None, :], (128, 1536))).astype(f32)
    gains = np.stack([inp["ffn1_norm"][0], inp["mix_norm"][0], inp["ffn2_norm"][0], inp["final_norm"]])
    sh["gains"] = np.ascontiguousarray(gains.reshape(4, KC, 128).transpose(2, 0, 1).reshape(128, 4 * KC))
    return sh


def _rope_tables(pos):
    inv_freq = (np.float32(500000.0) ** (-np.arange(0, 32, 2, dtype=np.float32) / np.float32(32))).astype(np.float32)
    ang = (pos.astype(np.float32)[None, :] * inv_freq[:, None]).astype(np.float32)
    cos = np.cos(ang.astype(np.float64)).astype(np.float32)
    sin = np.sin(ang.astype(np.float64)).astype(np.float32)
    return np.concatenate([cos, cos], 0), np.concatenate([sin, sin], 0)


def make_in_maps(inp):
    x = inp["x"]
    sh = _prep_shared(inp)
    cbase = _consts()
    in_maps = []
    for c in range(8):
        b, half = c // 2, c % 2
        own = x[b, half * TOWN:(half + 1) * TOWN]
        prev = x[b, 0:TOWN]
        xT = np.ascontiguousarray(np.concatenate([prev.T, own.T], axis=1))
        if half == 1:
            pos = np.arange(0, TALL)
        else:
            pos = np.concatenate([np.arange(0, TOWN), np.arange(0, TOWN)])
        cosT, sinT = _rope_tables(pos)
        cst = cbase.copy()
        cst[:, 384:512] = cbase[:, 256:384] if half == 1 else NEG
        m = dict(sh)
        m.update({"xT": xT, "cosT": np.ascontiguousarray(cosT), "sinT": np.ascontiguousarray(sinT), "cst": cst})
        in_maps.append(m)
    return in_maps


def kernel(**inputs):
    inp = {k: np.asarray(v) for k, v in inputs.items()}
    if "nc" not in _PROG:
        _PROG["nc"] = build_program()
    nc = _PROG["nc"]
    in_maps = make_in_maps(inp)
    res = run_bass_kernel_spmd(nc, in_maps, core_ids=list(range(8)))
    out = np.empty((4, 4096, D), np.float32)
    for c in range(8):
        b, half = c // 2, c % 2
        out[b, half * TOWN:(half + 1) * TOWN] = res.results[c]["outT"].T
    return out
```

```python
import os
import numpy as np
import ml_dtypes
import concourse.bass as bass
import concourse.mybir as mybir
from concourse.bass_utils import run_bass_kernel_spmd

F32 = mybir.dt.float32
BF16 = mybir.dt.bfloat16
AF = mybir.ActivationFunctionType
ALU = mybir.AluOpType

D = 2048
DFF = 5632
NFF = DFF // 128
KC = D // 128
TOWN = 2048
TALL = 4096
NEG = -30000.0
ENGS = ("pe", "act", "dve", "pool", "sp")
NDMA = 8


class Buf:
    __slots__ = ("name", "w", "r", "excl")

    def __init__(self, name="", excl=False):
        self.name = name
        self.w = None
        self.r = {}
        self.excl = excl


class Sched:
    def __init__(self, nc):
        self.nc = nc
        self.ops = {e: [] for e in ENGS}
        self.sem = {e: nc.alloc_semaphore(name="c_" + e) for e in ENGS}
        self.cnt = {e: 0 for e in ENGS}
        self.seen = {e: {} for e in ENGS}
        self.dq = ("sp", "act", "pool")
        self.dsem = {q: [nc.alloc_semaphore(name="d_%s%d" % (q, i)) for i in range(NDMA)]
                     for q in self.dq}
        self.dcnt = {q: [0] * NDMA for q in self.dq}
        self.dnext = {q: 0 for q in self.dq}

    def _need(self, e, tok, out):
        if tok is None:
            return
        sem, val = tok
        k = sem.name
        if self.seen[e].get(k, 0) >= val:
            return
        if out.get(k, (None, 0))[1] < val:
            out[k] = (sem, val)

    def _gather(self, e, reads, writes):
        need = {}
        for b in reads:
            self._need(e, b.w, need)
            if b.excl:
                for k, t in b.r.items():
                    if k != self.sem[e].name:
                        self._need(e, t, need)
        for b in writes:
            self._need(e, b.w, need)
            for t in b.r.values():
                self._need(e, t, need)
        return need

    def _emit_waits(self, e, need):
        lst = list(need.values())
        for sem, val in lst:
            self.seen[e][sem.name] = val
        if lst:
            self.ops[e].append(("wait", lst))

    def _commit(self, tok, reads, writes):
        k = tok[0].name
        for b in reads:
            if b.r.get(k, (None, 0))[1] < tok[1]:
                b.r[k] = tok
        for b in writes:
            b.w = tok
            b.r = {}

    def op(self, e, fn, reads=(), writes=()):
        need = self._gather(e, reads, writes)
        if e == "pe":
            need.pop(self.sem["pe"].name, None)
        self._emit_waits(e, need)
        self.cnt[e] += 1
        tok = (self.sem[e], self.cnt[e])
        self.ops[e].append(("op", fn, tok))
        self._commit(tok, reads, writes)
        return tok

    def dma(self, q, fn, reads=(), writes=()):
        need = self._gather(q, reads, writes)
        i = self.dnext[q]
        self.dnext[q] = (i + 1) % NDMA
        sem = self.dsem[q][i]
        if self.dcnt[q][i] > 0:
            self._need(q, (sem, self.dcnt[q][i]), need)
        self._emit_waits(q, need)
        self.dcnt[q][i] += 16
        tok = (sem, self.dcnt[q][i])
        self.ops[q].append(("dma", fn, tok))
        self._commit(tok, reads, writes)
        return tok

    def barrier(self):
        toks = [(self.sem[e], self.cnt[e]) for e in ENGS if self.cnt[e] > 0]
        for q in self.dq:
            for i in range(NDMA):
                if self.dcnt[q][i] > 0:
                    toks.append((self.dsem[q][i], self.dcnt[q][i]))
        for e in ENGS:
            need = {}
            for t in toks:
                if e == "pe" and t[0].name == self.sem["pe"].name:
                    continue
                self._need(e, t, need)
            self._emit_waits(e, need)

    def emit(self):
        nc = self.nc
        with nc.Block() as block:
            def run(e, engine):
                for item in self.ops[e]:
                    if item[0] == "wait":
                        for sem, val in item[1]:
                            engine.wait_ge(sem, val)
                    elif item[0] == "op":
                        item[1](engine).then_inc(item[2][0], 1)
                    else:
                        item[1](engine).then_inc(item[2][0], 16)

            @block.tensor
            def _(eng):
                run("pe", eng)

            @block.scalar
            def _(eng):
                run("act", eng)

            @block.vector
            def _(eng):
                run("dve", eng)

            @block.gpsimd
            def _(eng):
                run("pool", eng)

            @block.sync
            def _(eng):
                run("sp", eng)


class Ring:
    def __init__(self, items):
        self.items = items
        self.i = 0

    def next(self):
        it = self.items[self.i]
        self.i = (self.i + 1) % len(self.items)
        return it


def build_program(debug=False, stages=("ffn1", "qkv", "att", "mix", "ffn2", "final"), ffn1_tiles=4):
    nc = bass.Bass("TRN2", target_bir_lowering=False)
    S = Sched(nc)

    def din(name, shape, dt=F32):
        return nc.dram_tensor(name, list(shape), dt, kind="ExternalInput").ap()

    def dscr(name, shape, dt):
        return nc.dram_tensor(name, list(shape), dt, kind="ExternalOutput" if debug else "Internal").ap()

    XT = din("xT", [D, TALL])
    COS = din("cosT", [32, TALL])
    SIN = din("sinT", [32, TALL])
    GAINS = din("gains", [128, 4 * KC])
    WGU = [din("wgu1", [NFF, 128, 2 * KC * 128]), din("wgu2", [NFF, 128, 2 * KC * 128])]
    WD = [din("wd1", [KC, 128, NFF * 128]), din("wd2", [KC, 128, NFF * 128])]
    WIN = din("win", [92, 128, KC * 128])
    WINV = din("winv", [3, 128, KC * 512])
    WINVS = din("winvs", [3, 128, KC * 512])
    WATT = din("watt", [KC, 128, 4 * 128])
    WSG = din("wsg", [KC, 128, 12 * 128])
    WOUT = din("wout", [KC, 128, KC * 128])
    SGWT = din("sgwT", [128, 12 * 128])
    SGB = din("sgb", [1, 12 * 128])
    LNG = din("lng", [128, 1536])
    LNB = din("lnb", [128, 1536])
    CST = din("cst", [128, 1024])
    OUT = nc.dram_tensor("outT", [D, TOWN], F32, kind="ExternalOutput").ap()

    X1 = dscr("x1", [D, TALL], F32)
    X2 = dscr("x2", [D, TOWN], F32)
    X3 = dscr("x3", [D, TOWN], F32)
    H2 = dscr("h2", [D, TOWN], BF16)
    QTd = dscr("qt", [12, 128, TOWN], BF16)
    KTd = dscr("kt", [12, 128, TALL], BF16)
    Vd = dscr("vv", [3, TALL, 512], BF16)
    OTd = dscr("ot", [4, 128, TOWN], BF16)

    ARENA_F = 51200
    arena = nc.alloc_sbuf_tensor("arena", [128, ARENA_F], F32)
    st = {"off": 0}

    def carve(n_elems, dt):
        nbytes = n_elems * (4 if dt == F32 else 2)
        nw = (nbytes + 3) // 4
        nw = (nw + 7) // 8 * 8
        o = st["off"]
        st["off"] = o + nw
        assert st["off"] <= ARENA_F, ("SBUF arena overflow", st["off"])
        a = arena[:, o:o + nw]
        if dt == BF16:
            a = a.bitcast(BF16)
            return a[:, 0:n_elems]
        return a[:, 0:n_elems]

    psr = Ring([(nc.alloc_psum_tensor("ps%d" % i, [128, 512], F32), Buf("ps%d" % i, excl=True)) for i in range(8)])

    cst = carve(1024, F32)
    b_cst = Buf("cst")
    gains = carve(4 * KC, F32)
    ident_b = carve(128, BF16)
    maskCP_b = carve(256, BF16)
    maskP0_b = carve(128, BF16)
    rot_b = carve(128, BF16)
    ones_b = carve(128, BF16)
    b_cb = Buf("cbf")
    S.dma("sp", lambda e: e.dma_start(out=cst, in_=CST), writes=[b_cst])
    S.dma("sp", lambda e: e.dma_start(out=gains, in_=GAINS), writes=[b_cst])
    ident_f = cst[:, 0:128]
    ones_f = cst[:, 768:896]
    caus_f = cst[:, 640:768]
    S.op("dve", lambda e: e.tensor_copy(out=ident_b, in_=cst[:, 0:128]), reads=[b_cst], writes=[b_cb])
    S.op("dve", lambda e: e.tensor_copy(out=maskCP_b, in_=cst[:, 128:384]), reads=[b_cst], writes=[b_cb])
    S.op("dve", lambda e: e.tensor_copy(out=maskP0_b, in_=cst[:, 384:512]), reads=[b_cst], writes=[b_cb])
    S.op("dve", lambda e: e.tensor_copy(out=rot_b, in_=cst[:, 512:640]), reads=[b_cst], writes=[b_cb])
    S.op("dve", lambda e: e.tensor_copy(out=ones_b, in_=cst[:, 768:896]), reads=[b_cst], writes=[b_cb])
    base_off = st["off"]

    def tile_ring(n, elems, dt, name):
        return Ring([(carve(elems, dt), Buf("%s%d" % (name, i))) for i in range(n)])

    def norm_to_h(xsrc, c0, ntok, gidx, hT, hbufs, xring, sqring, rstd_t):
        for half in range(ntok // 512):
            col = c0 + half * 512
            ps, pb = psr.next()
            for kc in range(KC):
                xt, xb = xring.next()
                S.dma("sp", lambda e, xt=xt, kc=kc, col=col: e.dma_start(
                    out=xt, in_=xsrc[kc * 128:(kc + 1) * 128, col:col + 512]), writes=[xb])
                sq, sb = sqring.next()
                S.op("act", lambda e, sq=sq, xt=xt: e.activation(out=sq, in_=xt, func=AF.Square),
                     reads=[xb], writes=[sb])
                S.op("pe", lambda e, ps=ps, sq=sq, kc=kc: e.matmul(
                    ps[:], lhsT=ones_f, rhs=sq, start=(kc == 0), stop=(kc == KC - 1)),
                    reads=[sb, b_cst], writes=[pb])
            rs, rb = rstd_t
            S.op("dve", lambda e, ps=ps: e.tensor_scalar(out=rs, in0=ps[:], scalar1=1.0 / D, scalar2=1e-6,
                                                         op0=ALU.mult, op1=ALU.add), reads=[pb], writes=[rb])
            S.op("act", lambda e: e.activation(out=rs, in_=rs, func=AF.Sqrt), reads=[rb], writes=[rb])
            S.op("dve", lambda e: e.reciprocal(out=rs, in_=rs), reads=[rb], writes=[rb])
            for kc in range(KC):
                xt, xb = xring.next()
                S.dma("sp", lambda e, xt=xt, kc=kc, col=col: e.dma_start(
                    out=xt, in_=xsrc[kc * 128:(kc + 1) * 128, col:col + 512]), writes=[xb])
                dst = hT[:, kc * ntok + half * 512: kc * ntok + half * 512 + 512]
                S.op("dve", lambda e, dst=dst, xt=xt, kc=kc: e.scalar_tensor_tensor(
                    out=dst, in0=xt, scalar=gains[:, gidx * KC + kc: gidx * KC + kc + 1], in1=rs,
                    op0=ALU.mult, op1=ALU.mult), reads=[xb, rb, b_cst], writes=[hbufs[half]])

    def wload(dst, src, wb):
        S.dma("pool", lambda e: e.dma_start(out=dst, in_=src, max_dma_last_dim=8192), writes=[wb])

    def ffn_stages(which, xsrc, xdst, cols):
        st["off"] = base_off
        NT = 1024
        hT = carve(KC * NT, BF16)
        hbufs = [Buf("h0"), Buf("h1")]
        aT = carve(NFF * NT, BF16)
        abufs = [Buf("a0"), Buf("a1")]
        wring = tile_ring(3, NFF * 128, BF16, "w")
        xring = tile_ring(8, 512, F32, "x")
        sqring = tile_ring(3, 512, F32, "sq")
        sgring = tile_ring(2, 512, F32, "sg")
        oring = tile_ring(2, 512, F32, "o")
        rstd_t = (carve(512, F32), Buf("rstd"))
        wgu, wd = WGU[which], WD[which]
        gidx = 0 if which == 0 else 2
        for (c0, oc0) in cols:
            norm_to_h(xsrc, c0, NT, gidx, hT, hbufs, xring, sqring, rstd_t)
            for j in range(NFF):
                wt, wb = wring.next()
                wload(wt[:, 0:2 * KC * 128], wgu[j], wb)
                for t in range(2):
                    pg, pgb = psr.next()
                    pu, pub = psr.next()

                    def mm(e, ps, m, t=t, wt=wt):
                        ins = None
                        for kc in range(KC):
                            ins = e.matmul(ps[:], lhsT=wt[:, (m * KC + kc) * 128:(m * KC + kc + 1) * 128],
                                           rhs=hT[:, kc * NT + t * 512: kc * NT + t * 512 + 512],
                                           start=(kc == 0), stop=(kc == KC - 1))
                        return ins
                    S.op("pe", lambda e, pg=pg, mm=mm: mm(e, pg, 0), reads=[wb, hbufs[t]], writes=[pgb])
                    S.op("pe", lambda e, pu=pu, mm=mm: mm(e, pu, 1), reads=[wb, hbufs[t]], writes=[pub])
                    sg, sgb_ = sgring.next()
                    S.op("act", lambda e, sg=sg, pg=pg: e.activation(out=sg, in_=pg[:], func=AF.Silu),
                         reads=[pgb], writes=[sgb_])
                    dst = aT[:, j * NT + t * 512: j * NT + t * 512 + 512]
                    S.op("dve", lambda e, dst=dst, pu=pu, sg=sg: e.tensor_tensor(
                        out=dst, in0=pu[:], in1=sg, op=ALU.mult), reads=[pub, sgb_], writes=[abufs[t]])
            for i in range(KC):
                wt, wb = wring.next()
                wload(wt[:, 0:NFF * 128], wd[i], wb)
                for t in range(2):
                    xt, xb = xring.next()
                    S.dma("sp", lambda e, xt=xt, i=i, t=t, c0=c0: e.dma_start(
                        out=xt, in_=xsrc[i * 128:(i + 1) * 128, c0 + t * 512: c0 + t * 512 + 512]), writes=[xb])
                    ps, pb = psr.next()

                    def mm(e, ps=ps, t=t, wt=wt):
                        ins = None
                        for jc in range(NFF):
                            ins = e.matmul(ps[:], lhsT=wt[:, jc * 128:(jc + 1) * 128],
                                           rhs=aT[:, jc * NT + t * 512: jc * NT + t * 512 + 512],
                                           start=(jc == 0), stop=(jc == NFF - 1))
                        return ins
                    S.op("pe", mm, reads=[wb, abufs[t]], writes=[pb])
                    ot, ob = oring.next()
                    S.op("dve", lambda e, ot=ot, ps=ps, xt=xt: e.scalar_tensor_tensor(
                        out=ot, in0=ps[:], scalar=0.5, in1=xt, op0=ALU.mult, op1=ALU.add),
                        reads=[pb, xb], writes=[ob])
                    S.dma("act", lambda e, ot=ot, i=i, t=t, oc0=oc0: e.dma_start(
                        out=xdst[i * 128:(i + 1) * 128, oc0 + t * 512: oc0 + t * 512 + 512], in_=ot), reads=[ob])
        S.barrier()

    def qkv_stages():
        st["off"] = base_off
        NT = 1024
        hT = carve(KC * NT, BF16)
        hbufs = [Buf("h0"), Buf("h1")]
        wring = tile_ring(3, KC * 128, BF16, "w")
        wvring = tile_ring(2, KC * 512, BF16, "wv")
        xring = tile_ring(8, 512, F32, "x")
        sqring = tile_ring(3, 512, F32, "sq")
        rstd_t = (carve(512, F32), Buf("rstd"))
        cosb = carve(NT, F32)
        sinb = carve(NT, F32)
        b_cs = Buf("cs")
        rawring = tile_ring(3, 512, BF16, "raw")
        t1ring = tile_ring(2, 512, F32, "t1")
        t2ring = tile_ring(2, 512, F32, "t2")
        vring = tile_ring(3, 512, BF16, "v")
        for sti in range(4):
            c0 = sti * NT
            own = sti >= 2
            norm_to_h(X1, c0, NT, 1, hT, hbufs, xring, sqring, rstd_t)
            S.dma("sp", lambda e, c0=c0: e.dma_start(out=cosb[0:32, :], in_=COS[:, c0:c0 + NT]), writes=[b_cs])
            S.dma("sp", lambda e, c0=c0: e.dma_start(out=sinb[0:32, :], in_=SIN[:, c0:c0 + NT]), writes=[b_cs])
            PARTS = os.environ.get("QKV_PARTS", "h2,qk,rope,st,v").split(",")
            if own and "h2" in PARTS:
                for kc in range(KC):
                    S.dma("sp", lambda e, kc=kc, c0=c0: e.dma_start(
                        out=H2[kc * 128:(kc + 1) * 128, c0 - TOWN: c0 - TOWN + NT],
                        in_=hT[:, kc * NT:(kc + 1) * NT]), reads=hbufs)
            for c in ((range(24) if own else range(12, 24)) if "qk" in PARTS else []):
                if own or c >= 20:
                    tlist = (0, 1)
                elif sti == 1:
                    tlist = (1,)
                else:
                    continue
                wt, wb = wring.next()
                wload(wt, WIN[c], wb)
                for t in tlist:
                    ps, pb = psr.next()

                    def mm(e, ps=ps, t=t, wt=wt):
                        ins = None
                        for kc in range(KC):
                            ins = e.matmul(ps[:], lhsT=wt[:, kc * 128:(kc + 1) * 128],
                                           rhs=hT[:, kc * NT + t * 512: kc * NT + t * 512 + 512],
                                           start=(kc == 0), stop=(kc == KC - 1))
                        return ins
                    S.op("pe", mm, reads=[wb, hbufs[t]], writes=[pb])
                    raw, rb = rawring.next()
                    S.op("act", lambda e, raw=raw, ps=ps: e.activation(out=raw, in_=ps[:], func=AF.Copy),
                         reads=[pb], writes=[rb])
                    if "rope" not in PARTS:
                        continue
                    p2, p2b = psr.next()
                    S.op("pe", lambda e, p2=p2, raw=raw: e.matmul(p2[:], lhsT=rot_b, rhs=raw, start=True, stop=True),
                         reads=[rb, b_cb], writes=[p2b])
                    t1, t1b = t1ring.next()
                    t2, t2b = t2ring.next()
                    S.op("dve", lambda e, t1=t1, ps=ps, t=t: e.tensor_tensor(
                        out=t1[0:32, :], in0=ps[0:32, :], in1=cosb[0:32, t * 512:(t + 1) * 512], op=ALU.mult),
                        reads=[pb, b_cs], writes=[t1b])
                    S.op("dve", lambda e, t2=t2, p2=p2, t=t: e.tensor_tensor(
                        out=t2[0:32, :], in0=p2[0:32, :], in1=sinb[0:32, t * 512:(t + 1) * 512], op=ALU.mult),
                        reads=[p2b, b_cs], writes=[t2b])
                    S.op("dve", lambda e, raw=raw, t1=t1, t2=t2: e.tensor_tensor(
                        out=raw[0:32, :], in0=t1[0:32, :], in1=t2[0:32, :], op=ALU.add),
                        reads=[t1b, t2b], writes=[rb])
                    if c < 12:
                        dst = QTd[c, :, c0 - TOWN + t * 512: c0 - TOWN + t * 512 + 512]
                    else:
                        dst = KTd[c - 12, :, c0 + t * 512: c0 + t * 512 + 512]
                    if "st" in PARTS:
                        S.dma("sp", lambda e, dst=dst, raw=raw: e.dma_start(out=dst, in_=raw), reads=[rb])
            for g in (range(3) if "v" in PARTS else []):
                if own or g == 2:
                    tbs = range(NT // 128)
                elif sti == 1:
                    tbs = range(4, 8)
                else:
                    continue
                wv, wvb = wvring.next()
                wload(wv, WINV[g], wvb)
                for tb in tbs:
                    ps, pb = psr.next()

                    def mm(e, ps=ps, tb=tb, wv=wv):
                        ins = None
                        for kc in range(KC):
                            ins = e.matmul(ps[:], lhsT=hT[:, kc * NT + tb * 128: kc * NT + tb * 128 + 128],
                                           rhs=wv[:, kc * 512:(kc + 1) * 512],
                                           start=(kc == 0), stop=(kc == KC - 1))
                        return ins
                    S.op("pe", mm, reads=[wvb, hbufs[tb // 4]], writes=[pb])
                    vt, vb = vring.next()
                    S.op("act", lambda e, vt=vt, ps=ps: e.activation(out=vt, in_=ps[:], func=AF.Copy),
                         reads=[pb], writes=[vb])
                    S.dma("sp", lambda e, vt=vt, g=g, tb=tb, c0=c0: e.dma_start(
                        out=Vd[g, c0 + tb * 128: c0 + tb * 128 + 128, :], in_=vt), reads=[vb])
        S.barrier()

    def att_stage():
        st["off"] = base_off
        Vr = carve(32 * 512, BF16)
        b_vr = Buf("vr")
        qn_r = tile_ring(2, TOWN, BF16, "qn")
        qr_r = tile_ring(2, TOWN, BF16, "qr")
        kn_r = tile_ring(2, TALL, BF16, "kn")
        kr_r = tile_ring(2, TALL, BF16, "kr")
        numacc = [carve(TOWN, F32) for _ in range(4)]
        denacc = [carve(TOWN, F32) for _ in range(4)]
        b_acc = [Buf("acc%d" % j) for j in range(4)]
        pring = tile_ring(4, 256, BF16, "p")
        obf = carve(TOWN, BF16)
        b_obf = Buf("obf")
        scale = 1.0 / np.sqrt(128.0)
        ps_s = Ring(psr.items[0:4])
        ps_nd = Ring(psr.items[4:8])
        for g, dil in enumerate((1, 4, 16)):
            nkb = TALL // (128 * dil)
            nbq = TOWN // (128 * dil)
            Lq = TOWN // dil
            Lk = TALL // dil
            vsrc = Vd[g].rearrange("(kb i r) c -> r i kb c", r=dil, i=128)
            for r in range(dil):
                S.dma("sp", lambda e, r=r, vsrc=vsrc, nkb=nkb: e.dma_start(
                    out=Vr[:, r * nkb * 512:(r + 1) * nkb * 512].rearrange("p (k c) -> p k c", c=512),
                    in_=vsrc[r]), writes=[b_vr])
            for j in range(4):
                h = 4 * g + j
                qn, qnb = qn_r.next()
                kn, knb = kn_r.next()
                S.dma("sp", lambda e, qn=qn, h=h: e.dma_start(out=qn, in_=QTd[h]), writes=[qnb])
                S.dma("sp", lambda e, kn=kn, h=h: e.dma_start(out=kn, in_=KTd[h]), writes=[knb])
                if dil == 1:
                    qr, qrb, kr, krb = qn, qnb, kn, knb
                else:
                    qr, qrb = qr_r.next()
                    kr, krb = kr_r.next()
                    S.op("pool", lambda e, qr=qr, qn=qn, dil=dil: e.tensor_copy(
                        out=qr.rearrange("p (r i) -> p r i", r=dil),
                        in_=qn.rearrange("p (i r) -> p r i", r=dil)), reads=[qnb], writes=[qrb])
                    S.op("pool", lambda e, kr=kr, kn=kn, dil=dil: e.tensor_copy(
                        out=kr.rearrange("p (r i) -> p r i", r=dil),
                        in_=kn.rearrange("p (i r) -> p r i", r=dil)), reads=[knb], writes=[krb])
                qblocks = [(r, m) for r in range(dil) for m in range(nbq)]
                def issue_scores(qi):
                        r, m = qblocks[qi]
                        qcol = r * Lq + m * 128
                        parts = []
                        for role, kb in (("prev", nbq + m - 1), ("cur", nbq + m)):
                            ps, pb = ps_s.next()
                            kcol = r * Lk + kb * 128
                            if role == "cur":
                                mk = maskCP_b[:, 0:128]
                            elif m == 0:
                                mk = maskP0_b
                            else:
                                mk = maskCP_b[:, 128:256]

                            def mms(e, ps=ps, kcol=kcol, qcol=qcol, mk=mk, kr=kr, qr=qr):
                                e.matmul(ps[:, 0:128], lhsT=kr[:, kcol:kcol + 128], rhs=qr[:, qcol:qcol + 128],
                                         start=True, stop=False)
                                return e.matmul(ps[:, 0:128], lhsT=ident_b, rhs=mk, start=False, stop=True)
                            S.op("pe", mms, reads=[krb, qrb, b_cb], writes=[pb])
                            pt, ptb = pring.next()
                            S.op("act", lambda e, pt=pt, ps=ps: e.activation(
                                out=pt[:, 0:128], in_=ps[:, 0:128], func=AF.Exp, scale=float(scale)),
                                reads=[pb], writes=[ptb])
                            parts.append((pt, ptb, r * nkb + kb))
                        return parts

                nxt = issue_scores(0)
                for q0 in range(0, len(qblocks), 4):
                    pn, pnb = ps_nd.next()
                    pd, pdb = ps_nd.next()
                    for s4 in range(4):
                        parts = nxt
                        if q0 + s4 + 1 < len(qblocks):
                            nxt = issue_scores(q0 + s4 + 1)

                        def pv(e, pn=pn, pd=pd, s4=s4, parts=parts, j=j):
                            ins = None
                            for idx, (pt, ptb, blk) in enumerate(parts):
                                e.matmul(pn[:, s4 * 128:(s4 + 1) * 128],
                                         lhsT=Vr[:, blk * 512 + j * 128: blk * 512 + (j + 1) * 128],
                                         rhs=pt[:, 0:128], start=(idx == 0), stop=(idx == 1))
                            for idx, (pt, ptb, blk) in enumerate(parts):
                                ins = e.matmul(pd[:, s4 * 128:(s4 + 1) * 128], lhsT=ones_b, rhs=pt[:, 0:128],
                                               start=(idx == 0), stop=(idx == 1))
                            return ins
                        S.op("pe", pv, reads=[parts[0][1], parts[1][1], b_vr, b_cb], writes=[pnb, pdb])
                    r0, m0 = qblocks[q0]
                    if dil == 1:
                        tok = slice(m0 * 128, m0 * 128 + 512)
                        na, da = numacc[j][:, tok], denacc[j][:, tok]
                        pna, pda = pn[:], pd[:]
                    elif dil == 4:
                        na = numacc[j].rearrange("p (i r) -> p r i", r=4)[:, r0, :]
                        da = denacc[j].rearrange("p (i r) -> p r i", r=4)[:, r0, :]
                        pna, pda = pn[:], pd[:]
                    else:
                        na = numacc[j].rearrange("p (i r) -> p r i", r=16)[:, r0:r0 + 4, :]
                        da = denacc[j].rearrange("p (i r) -> p r i", r=16)[:, r0:r0 + 4, :]
                        pna = pn[:].rearrange("p (s i) -> p s i", s=4)
                        pda = pd[:].rearrange("p (s i) -> p s i", s=4)
                    if g == 0:
                        S.op("dve", lambda e, na=na, pna=pna: e.tensor_copy(out=na, in_=pna),
                             reads=[pnb], writes=[b_acc[j]])
                        S.op("dve", lambda e, da=da, pda=pda: e.tensor_copy(out=da, in_=pda),
                             reads=[pdb], writes=[b_acc[j]])
                    else:
                        S.op("dve", lambda e, na=na, pna=pna: e.tensor_tensor(out=na, in0=pna, in1=na, op=ALU.add),
                             reads=[pnb], writes=[b_acc[j]])
                        S.op("dve", lambda e, da=da, pda=pda: e.tensor_tensor(out=da, in0=pda, in1=da, op=ALU.add),
                             reads=[pdb], writes=[b_acc[j]])
        for j in range(4):
            S.op("dve", lambda e, j=j: e.reciprocal(out=denacc[j], in_=denacc[j]), writes=[b_acc[j]])
            S.op("dve", lambda e, j=j: e.tensor_tensor(out=obf, in0=numacc[j], in1=denacc[j], op=ALU.mult),
                 reads=[b_acc[j]], writes=[b_obf])
            S.dma("sp", lambda e, j=j: e.dma_start(out=OTd[j], in_=obf), reads=[b_obf])
        S.barrier()

    def mix_stages():
        st["off"] = base_off
        NT = 512
        h2 = carve(KC * NT, BF16)
        b_h2 = Buf("h2")
        oT = carve(4 * NT, BF16)
        b_oT = Buf("oT")
        uT = carve(12 * NT, F32)
        b_uT = Buf("uT")
        vsn = [carve(1536, BF16) for _ in range(4)]
        b_vsn = [Buf("vsn%d" % i) for i in range(4)]
        sgT = carve(12 * NT, BF16)
        b_sgT = Buf("sgT")
        lng = carve(1536, F32)
        lnb = carve(1536, F32)
        wsp = carve(12 * 128, BF16)
        sgb = carve(12 * 128, F32)
        b_c2 = Buf("c2")
        stats = carve(18, F32)
        mv = carve(2, F32)
        rstd = carve(1, F32)
        b_st = Buf("st")
        wring = tile_ring(6, KC * 128, BF16, "w")
        tmpw = carve(12 * 128, F32)
        S.dma("sp", lambda e: e.dma_start(out=lng, in_=LNG), writes=[b_c2])
        S.dma("sp", lambda e: e.dma_start(out=lnb, in_=LNB), writes=[b_c2])
        S.dma("sp", lambda e: e.dma_start(out=sgb[0:1, :], in_=SGB), writes=[b_c2])
        S.dma("sp", lambda e: e.dma_start(out=tmpw, in_=SGWT), writes=[b_c2])
        for gq in range(12):
            S.op("dve", lambda e, gq=gq: e.tensor_tensor(
                out=wsp[:, gq * 128:(gq + 1) * 128], in0=tmpw[:, gq * 128:(gq + 1) * 128],
                in1=caus_f, op=ALU.mult), reads=[b_c2, b_cst], writes=[b_c2])
        alias0 = st["off"]
        for sti in range(TOWN // NT):
            oc0 = sti * NT
            st["off"] = alias0
            wvs = [carve(KC * 512, BF16) for _ in range(3)]
            b_wvs = [Buf("wvs%d" % i) for i in range(3)]
            gv = carve(1536, F32)
            b_gv = Buf("gv")
            for kc in range(KC):
                S.dma("sp", lambda e, kc=kc, oc0=oc0: e.dma_start(
                    out=h2[:, kc * NT:(kc + 1) * NT], in_=H2[kc * 128:(kc + 1) * 128, oc0:oc0 + NT]), writes=[b_h2])
            for j in range(4):
                S.dma("sp", lambda e, j=j, oc0=oc0: e.dma_start(
                    out=oT[:, j * NT:(j + 1) * NT], in_=OTd[j, :, oc0:oc0 + NT]), writes=[b_oT])
            for ct in range(3):
                wload(wvs[ct], WINVS[ct], b_wvs[ct])
            for c in range(12):
                wt, wb = wring.next()
                wload(wt, WIN[36 + c], wb)
                ps, pb = psr.next()

                def mm(e, ps=ps, wt=wt):
                    ins = None
                    for kc in range(KC):
                        ins = e.matmul(ps[:], lhsT=wt[:, kc * 128:(kc + 1) * 128], rhs=h2[:, kc * NT:(kc + 1) * NT],
                                       start=(kc == 0), stop=(kc == KC - 1))
                    return ins
                S.op("pe", mm, reads=[wb, b_h2], writes=[pb])
                S.op("act", lambda e, c=c, ps=ps: e.activation(out=uT[:, c * NT:(c + 1) * NT], in_=ps[:], func=AF.Gelu),
                     reads=[pb], writes=[b_uT])
            for tb in range(4):
                for ct in range(3):
                    ps, pb = psr.next()

                    def mm(e, ps=ps, ct=ct, tb=tb):
                        ins = None
                        for kc in range(KC):
                            ins = e.matmul(ps[:], lhsT=h2[:, kc * NT + tb * 128: kc * NT + tb * 128 + 128],
                                           rhs=wvs[ct][:, kc * 512:(kc + 1) * 512],
                                           start=(kc == 0), stop=(kc == KC - 1))
                        return ins
                    S.op("pe", mm, reads=[b_wvs[ct], b_h2], writes=[pb])
                    S.op("act", lambda e, ps=ps, ct=ct: e.activation(
                        out=gv[:, ct * 512:(ct + 1) * 512], in_=ps[:], func=AF.Gelu), reads=[pb], writes=[b_gv])
                for ct in range(3):
                    S.op("dve", lambda e, ct=ct: e.bn_stats(out=stats[:, ct * 6:(ct + 1) * 6],
                                                            in_=gv[:, ct * 512:(ct + 1) * 512]),
                         reads=[b_gv], writes=[b_st])
                S.op("dve", lambda e: e.bn_aggr(out=mv, in_=stats), reads=[b_st], writes=[b_st])
                S.op("dve", lambda e: e.tensor_scalar(out=rstd, in0=mv[:, 1:2], scalar1=1e-5, scalar2=1.0,
                                                      op0=ALU.add, op1=ALU.mult), reads=[b_st], writes=[b_st])
                S.op("act", lambda e: e.activation(out=rstd, in_=rstd, func=AF.Sqrt), reads=[b_st], writes=[b_st])
                S.op("dve", lambda e: e.reciprocal(out=rstd, in_=rstd), reads=[b_st], writes=[b_st])
                S.op("dve", lambda e: e.tensor_scalar(out=gv, in0=gv, scalar1=mv[:, 0:1], scalar2=rstd[:, 0:1],
                                                      op0=ALU.subtract, op1=ALU.mult), reads=[b_st, b_gv], writes=[b_gv])
                S.op("dve", lambda e: e.tensor_tensor(out=gv, in0=gv, in1=lng, op=ALU.mult),
                     reads=[b_gv, b_c2], writes=[b_gv])
                S.op("dve", lambda e, tb=tb: e.tensor_tensor(out=vsn[tb], in0=gv, in1=lnb, op=ALU.add),
                     reads=[b_gv, b_c2], writes=[b_vsn[tb]])
            for grp in range(12):
                ps, pb = psr.next()

                def mm(e, ps=ps, grp=grp):
                    ins = None
                    for tb in range(4):
                        e.matmul(ps[:, tb * 128:(tb + 1) * 128], lhsT=ones_f[0:1, :],
                                 rhs=sgb[0:1, grp * 128:(grp + 1) * 128], start=True, stop=False)
                        ins = e.matmul(ps[:, tb * 128:(tb + 1) * 128], lhsT=vsn[tb][:, grp * 128:(grp + 1) * 128],
                                       rhs=wsp[:, grp * 128:(grp + 1) * 128], start=False, stop=True)
                    return ins
                S.op("pe", mm, reads=b_vsn + [b_c2, b_cst], writes=[pb])
                S.op("dve", lambda e, ps=ps, grp=grp: e.tensor_tensor(
                    out=sgT[:, grp * NT:(grp + 1) * NT], in0=ps[:], in1=uT[:, grp * NT:(grp + 1) * NT], op=ALU.mult),
                    reads=[pb, b_uT], writes=[b_sgT])
            S.barrier()
            st["off"] = alias0
            merged = carve(KC * NT, BF16)
            b_mg = Buf("mg")
            sa_r = tile_ring(2, NT, F32, "sa")
            ss_r = tile_ring(2, NT, F32, "ss")
            m1_r = tile_ring(2, NT, F32, "m1")
            m2_r = tile_ring(2, NT, F32, "m2")
            x_r = tile_ring(3, NT, F32, "xr")
            o_r = tile_ring(2, NT, F32, "or")
            w2ring = tile_ring(2, 12 * 128, BF16, "w2")
            w3ring = tile_ring(2, 4 * 128, BF16, "w3")
            for i in range(KC):
                wa, wab = wring.next()
                wload(wa, WIN[60 + i], wab)
                ws, wsb = wring.next()
                wload(ws, WIN[76 + i], wsb)
                w2, w2b = w2ring.next()
                wload(w2, WSG[i], w2b)
                w3, w3b = w3ring.next()
                wload(w3, WATT[i], w3b)
                pga, pgab = psr.next()
                pgs, pgsb = psr.next()
                pya, pyab = psr.next()
                pys, pysb = psr.next()

                def mmg(e, ps, wt):
                    ins = None
                    for kc in range(KC):
                        ins = e.matmul(ps[:], lhsT=wt[:, kc * 128:(kc + 1) * 128], rhs=h2[:, kc * NT:(kc + 1) * NT],
                                       start=(kc == 0), stop=(kc == KC - 1))
                    return ins
                S.op("pe", lambda e, pga=pga, wa=wa, mmg=mmg: mmg(e, pga, wa), reads=[wab, b_h2], writes=[pgab])
                S.op("pe", lambda e, pgs=pgs, ws=ws, mmg=mmg: mmg(e, pgs, ws), reads=[wsb, b_h2], writes=[pgsb])

                def mmya(e, pya=pya, w3=w3):
                    ins = None
                    for j in range(4):
                        ins = e.matmul(pya[:], lhsT=w3[:, j * 128:(j + 1) * 128], rhs=oT[:, j * NT:(j + 1) * NT],
                                       start=(j == 0), stop=(j == 3))
                    return ins
                S.op("pe", mmya, reads=[w3b, b_oT], writes=[pyab])

                def mmys(e, pys=pys, w2=w2):
                    ins = None
                    for gq in range(12):
                        ins = e.matmul(pys[:], lhsT=w2[:, gq * 128:(gq + 1) * 128], rhs=sgT[:, gq * NT:(gq + 1) * NT],
                                       start=(gq == 0), stop=(gq == 11))
                    return ins
                S.op("pe", mmys, reads=[w2b, b_sgT], writes=[pysb])
                sa, sab = sa_r.next()
                ss, ssb = ss_r.next()
                S.op("act", lambda e, sa=sa, pga=pga: e.activation(out=sa, in_=pga[:], func=AF.Sigmoid),
                     reads=[pgab], writes=[sab])
                S.op("act", lambda e, ss=ss, pgs=pgs: e.activation(out=ss, in_=pgs[:], func=AF.Sigmoid),
                     reads=[pgsb], writes=[ssb])
                m1, m1b = m1_r.next()
                m2, m2b = m2_r.next()
                S.op("dve", lambda e, m1=m1, pya=pya, sa=sa: e.tensor_tensor(out=m1, in0=pya[:], in1=sa, op=ALU.mult),
                     reads=[pyab, sab], writes=[m1b])
                S.op("dve", lambda e, m2=m2, pys=pys, ss=ss: e.tensor_tensor(out=m2, in0=pys[:], in1=ss, op=ALU.mult),
                     reads=[pysb, ssb], writes=[m2b])
                S.op("pool", lambda e, i=i, m1=m1, m2=m2: e.tensor_tensor(
                    out=merged[:, i * NT:(i + 1) * NT], in0=m1, in1=m2, op=ALU.add),
                    reads=[m1b, m2b], writes=[b_mg])
            for i in range(KC):
                wo, wob = wring.next()
                wload(wo, WOUT[i], wob)
                xt, xb = x_r.next()
                S.dma("sp", lambda e, xt=xt, i=i, oc0=oc0: e.dma_start(
                    out=xt, in_=X1[i * 128:(i + 1) * 128, TOWN + oc0: TOWN + oc0 + NT]), writes=[xb])
                ps, pb = psr.next()

                def mm(e, ps=ps, wo=wo):
                    ins = None
                    for kc in range(KC):
                        ins = e.matmul(ps[:], lhsT=wo[:, kc * 128:(kc + 1) * 128], rhs=merged[:, kc * NT:(kc + 1) * NT],
                                       start=(kc == 0), stop=(kc == KC - 1))
                    return ins
                S.op("pe", mm, reads=[wob, b_mg], writes=[pb])
                ot, ob = o_r.next()
                S.op("dve", lambda e, ot=ot, ps=ps, xt=xt: e.tensor_tensor(out=ot, in0=ps[:], in1=xt, op=ALU.add),
                     reads=[pb, xb], writes=[ob])
                S.dma("act", lambda e, ot=ot, i=i, oc0=oc0: e.dma_start(
                    out=X2[i * 128:(i + 1) * 128, oc0:oc0 + NT], in_=ot), reads=[ob])
            S.barrier()

    def final_stage():
        st["off"] = base_off
        xring = tile_ring(8, 512, F32, "x")
        sqring = tile_ring(3, 512, F32, "sq")
        oring = tile_ring(3, 512, F32, "o")
        rs = carve(512, F32)
        rb = Buf("rstd")
        b_out = Buf("out")
        for tt in range(TOWN // 512):
            col = tt * 512
            ps, pb = psr.next()
            for kc in range(KC):
                xt, xb = xring.next()
                S.dma("sp", lambda e, xt=xt, kc=kc, col=col: e.dma_start(
                    out=xt, in_=X3[kc * 128:(kc + 1) * 128, col:col + 512]), writes=[xb])
                sq, sb = sqring.next()
                S.op("act", lambda e, sq=sq, xt=xt: e.activation(out=sq, in_=xt, func=AF.Square),
                     reads=[xb], writes=[sb])
                S.op("pe", lambda e, ps=ps, sq=sq, kc=kc: e.matmul(
                    ps[:], lhsT=ones_f, rhs=sq, start=(kc == 0), stop=(kc == KC - 1)),
                    reads=[sb, b_cst], writes=[pb])
            S.op("dve", lambda e, ps=ps: e.tensor_scalar(out=rs, in0=ps[:], scalar1=1.0 / D, scalar2=1e-6,
                                                         op0=ALU.mult, op1=ALU.add), reads=[pb], writes=[rb])
            S.op("act", lambda e: e.activation(out=rs, in_=rs, func=AF.Sqrt), reads=[rb], writes=[rb])
            S.op("dve", lambda e: e.reciprocal(out=rs, in_=rs), reads=[rb], writes=[rb])
            for kc in range(KC):
                xt, xb = xring.next()
                S.dma("sp", lambda e, xt=xt, kc=kc, col=col: e.dma_start(
                    out=xt, in_=X3[kc * 128:(kc + 1) * 128, col:col + 512]), writes=[xb])
                ot, ob = oring.next()
                S.op("dve", lambda e, ot=ot, xt=xt, kc=kc: e.scalar_tensor_tensor(
                    out=ot, in0=xt, scalar=gains[:, 3 * KC + kc: 3 * KC + kc + 1], in1=rs,
                    op0=ALU.mult, op1=ALU.mult), reads=[xb, rb, b_cst], writes=[ob])
                S.dma("act", lambda e, ot=ot, kc=kc, col=col: e.dma_start(
                    out=OUT[kc * 128:(kc + 1) * 128, col:col + 512], in_=ot), reads=[ob], writes=[b_out])
        S.barrier()

    if "ffn1" in stages:
        ffn_stages(0, XT, X1, [(0, 0), (1024, 1024), (2048, 2048), (3072, 3072)][4 - ffn1_tiles:])
    if "qkv" in stages:
        qkv_stages()
    if "att" in stages:
        att_stage()
    if "mix" in stages:
        mix_stages()
    if "ffn2" in stages:
        ffn_stages(1, X2, X3, [(0, 0), (1024, 1024)])
    if "final" in stages:
        final_stage()
    S.barrier()
    S.emit()
    return nc


_PROG = {}


def _consts():
    c = np.zeros((128, 1024), np.float32)
    p = np.arange(128)[:, None]
    f = np.arange(128)[None, :]
    c[:, 0:128] = np.eye(128, dtype=np.float32)
    c[:, 128:256] = np.where(p <= f, 0.0, NEG)
    c[:, 256:384] = np.where(p >= f, 0.0, NEG)
    rot = np.zeros((32, 32), np.float32)
    for i in range(16):
        rot[i + 16, i] = -1.0
        rot[i, i + 16] = 1.0
    c[0:32, 512:544] = rot
    c[:, 640:768] = (p <= f).astype(np.float32)
    c[:, 768:896] = 1.0
    return c


def _prep_shared(inp):
    f32 = np.float32

    def tile_w(w, ncol):
        K, N = w.shape
        return np.ascontiguousarray(
            w.reshape(K // 128, 128, N // ncol, ncol).transpose(2, 1, 0, 3).reshape(N // ncol, 128, (K // 128) * ncol))

    sh = {}
    for n, (wg, wu, wd) in enumerate((("ffn1_w_gate", "ffn1_w_up", "ffn1_w_down"),
                                      ("ffn2_w_gate", "ffn2_w_up", "ffn2_w_down"))):
        g = tile_w(inp[wg][0], 128)
        u = tile_w(inp[wu][0], 128)
        sh["wgu%d" % (n + 1)] = np.ascontiguousarray(np.concatenate([g, u], axis=2))
        sh["wd%d" % (n + 1)] = tile_w(inp[wd][0], 128)
    w_in = inp["w_in"][0]
    sh["win"] = tile_w(w_in, 128)
    sh["winv"] = tile_w(w_in[:, 3072:4608], 512)
    sh["winvs"] = tile_w(w_in[:, 6144:7680], 512)
    sh["watt"] = tile_w(inp["w_att_out"][0], 128)
    sh["wsg"] = tile_w(inp["w_sg_out"][0], 128)
    sh["wout"] = tile_w(inp["w_out"][0], 128)
    sgw = inp["sg_w"][0]
    sh["sgwT"] = np.ascontiguousarray(sgw.transpose(2, 0, 1).reshape(128, 12 * 128))
    sh["sgb"] = np.ascontiguousarray(inp["sg_b"][0].reshape(1, 12 * 128))
    sh["lng"] = np.ascontiguousarray(np.broadcast_to(inp["sg_ln_g"][0][None, :], (128, 1536))).astype(f32)
    sh["lnb"] = np.ascontiguousarray(np.broadcast_to(inp["sg_ln_b"][0][None, :], (128, 1536))).astype(f32)
    gains = np.stack([inp["ffn1_norm"][0], inp["mix_norm"][0], inp["ffn2_norm"][0], inp["final_norm"]])
    sh["gains"] = np.ascontiguousarray(gains.reshape(4, KC, 128).transpose(2, 0, 1).reshape(128, 4 * KC))
    return sh


def _rope_tables(pos):
    inv_freq = (np.float32(500000.0) ** (-np.arange(0, 32, 2, dtype=np.float32) / np.float32(32))).astype(np.float32)
    ang = (pos.astype(np.float32)[None, :] * inv_freq[:, None]).astype(np.float32)
    cos = np.cos(ang.astype(np.float64)).astype(np.float32)
    sin = np.sin(ang.astype(np.float64)).astype(np.float32)
    return np.concatenate([cos, cos], 0), np.concatenate([sin, sin], 0)


def make_in_maps(inp):
    x = inp["x"]
    sh = _prep_shared(inp)
    cbase = _consts()
    in_maps = []
    for c in range(8):
        b, half = c // 2, c % 2
        own = x[b, half * TOWN:(half + 1) * TOWN]
        prev = x[b, 0:TOWN]
        xT = np.ascontiguousarray(np.concatenate([prev.T, own.T], axis=1))
        if half == 1:
            pos = np.arange(0, TALL)
        else:
            pos = np.concatenate([np.arange(0, TOWN), np.arange(0, TOWN)])
        cosT, sinT = _rope_tables(pos)
        cst = cbase.copy()
        cst[:, 384:512] = cbase[:, 256:384] if half == 1 else NEG
        m = dict(sh)
        m.update({"xT": xT, "cosT": np.ascontiguousarray(cosT), "sinT": np.ascontiguousarray(sinT), "cst": cst})
        in_maps.append(m)
    return in_maps


def kernel(**inputs):
    inp = {k: np.asarray(v) for k, v in inputs.items()}
    if "nc" not in _PROG:
        _PROG["nc"] = build_program()
    nc = _PROG["nc"]
    in_maps = make_in_maps(inp)
    res = run_bass_kernel_spmd(nc, in_maps, core_ids=list(range(8)))
    out = np.empty((4, 4096, D), np.float32)
    for c in range(8):
        b, half = c // 2, c % 2
        out[b, half * TOWN:(half + 1) * TOWN] = res.results[c]["outT"].T
    return out
```
